# Optimizing a Trainium2 kernel written in Bass

```python
import jax
import jax.numpy as jnp
from jax import lax
import numpy as np

D_MODEL = 1024
BATCH = 16
SEQ = 256
DEPTH = 4
DEC_BATCH = 8
DEC_SEQ = 1024
PAST_LEN = 512

GRID_W = 64
N_MIX = 3
N_A = (DEPTH + 2) // 3
N_B = (DEPTH + 1) // 3
N_C = DEPTH // 3
CHUNK = 128
A_HALF = 2 * D_MODEL
A_GROUPS = 8
N_HEADS = 16
HEAD_DIM = D_MODEL // N_HEADS
WIN_ROWS_MAX = 8
WIN_COLS = 16
CONV_W = 3
D_FF = 4 * D_MODEL
EPS = 1e-6

kernel_name = 'hybrid_diffusion_gmlp_natten_shortconv_step'


def rms_norm(x, g):
    xf = x.astype(jnp.float32)
    y = xf * lax.rsqrt(jnp.mean(xf * xf, axis=-1, keepdims=True) + EPS)
    return (y * g.astype(jnp.float32)).astype(x.dtype)


def modulation(cond, w, b):
    m = jax.nn.silu(cond) @ w + b
    return jnp.split(m[:, None, :], 6, axis=-1)


def modulate(h, shift, scale):
    return h * (1 + scale) + shift


def sq_relu_mlp(h, w1, w2):
    return jnp.square(jax.nn.relu(h @ w1)) @ w2


def chunk_gmlp(h, w_in, v_gain, ws, bs, w_out):
    b, l, _ = h.shape
    z = jax.nn.gelu(h @ w_in)
    u, v = jnp.split(z, 2, axis=-1)
    v = rms_norm(v, v_gain).reshape(b, l // CHUNK, CHUNK, A_GROUPS, A_HALF // A_GROUPS)
    s = jnp.einsum('gpq,bnqgc->bnpgc', ws, v) + bs.T[None, None, :, :, None]
    return (u * s.reshape(b, l, A_HALF)) @ w_out


def short_gated_conv(h, w_in, conv_w, conv_b, w_out):
    bg, cg, xt = jnp.split(h @ w_in, 3, axis=-1)
    z = cg * xt
    zc = lax.conv_general_dilated(z, conv_w[:, None, :], window_strides=(1,), padding=((1, 1),),
                                  dimension_numbers=('NWC', 'WIO', 'NWC'),
                                  feature_group_count=D_MODEL) + conv_b
    return (bg * zc) @ w_out


def qkv_heads(h, w_qkv, q_gain, k_gain):
    b, l, _ = h.shape
    qkv = (h @ w_qkv).reshape(b, l, 3, N_HEADS, HEAD_DIM)
    return rms_norm(qkv[:, :, 0], q_gain), rms_norm(qkv[:, :, 1], k_gain), qkv[:, :, 2]


def context_attention(q, k, v):
    b, l, _, _ = q.shape
    s = jnp.einsum('bqhd,bkhd->bhqk', q, k).astype(jnp.float32) * (HEAD_DIM ** -0.5)
    p = jax.nn.softmax(s, axis=-1).astype(v.dtype)
    return jnp.einsum('bhqk,bkhd->bqhd', p, v).reshape(b, l, D_MODEL)


def neighbourhood_attention(q, k, v, ck, cv, rpb):
    b, l, _, _ = q.shape
    rows = l // GRID_W
    wr = min(WIN_ROWS_MAX, rows)
    r = jnp.arange(rows)
    row_idx = jnp.clip(r - wr // 2, 0, rows - wr)[:, None] + jnp.arange(wr)[None, :]
    col = jnp.arange(GRID_W)
    col_start = jnp.clip(col - WIN_COLS // 2, 0, GRID_W - WIN_COLS)
    col_ok = (col[None, :] >= col_start[:, None]) & (col[None, :] < col_start[:, None] + WIN_COLS)
    rel_r = row_idx - r[:, None] + (WIN_ROWS_MAX - 1)
    rel_c = jnp.clip(col[None, :] - col[:, None] + (WIN_COLS - 1), 0, 2 * WIN_COLS - 2)
    bias = rpb[:, rel_r[:, None, :, None], rel_c[None, :, None, :]]
    bias = jnp.moveaxis(bias, 0, 1).astype(jnp.float32)
    qg = q.reshape(b, rows, GRID_W, N_HEADS, HEAD_DIM)
    k_band = k.reshape(b, rows, GRID_W, N_HEADS, HEAD_DIM)[:, row_idx]
    v_band = v.reshape(b, rows, GRID_W, N_HEADS, HEAD_DIM)[:, row_idx]
    scale = HEAD_DIM ** -0.5
    s_loc = jnp.einsum('brqhd,brikhd->brhqik', qg, k_band).astype(jnp.float32) * scale + bias
    s_loc = jnp.where(col_ok[:, None, :], s_loc, -jnp.inf)
    s_ctx = jnp.einsum('brqhd,bkhd->brhqk', qg, ck).astype(jnp.float32) * scale
    n_loc = wr * GRID_W
    s_all = jnp.concatenate([s_loc.reshape(b, rows, N_HEADS, GRID_W, n_loc), s_ctx], axis=-1)
    p = jax.nn.softmax(s_all, axis=-1).astype(v.dtype)
    p_loc = p[..., :n_loc].reshape(b, rows, N_HEADS, GRID_W, wr, GRID_W)
    o = (jnp.einsum('brhqik,brikhd->brqhd', p_loc, v_band)
         + jnp.einsum('brhqk,bkhd->brqhd', p[..., n_loc:], cv))
    return o.reshape(b, l, D_MODEL)


def setup_inputs(seed: int = 0) -> dict:
    key = jax.random.key(seed)
    ks = jax.random.split(key, 26)
    D = D_MODEL

    def nrm(k, shape, scale):
        return jax.random.normal(k, shape, jnp.float32) * scale

    return {
        'x_prompt': nrm(ks[0], (BATCH, SEQ, D), 1.0),
        'x_sample': nrm(ks[1], (DEC_BATCH, DEC_SEQ, D), 1.0),
        'cache_k': nrm(ks[2], (DEC_BATCH, N_B, PAST_LEN, N_HEADS, HEAD_DIM), 1.0),
        'cache_v': nrm(ks[3], (DEC_BATCH, N_B, PAST_LEN, N_HEADS, HEAD_DIM), 1.0),
        'c': nrm(ks[4], (DEC_BATCH, D), 1.0),
        'c_ctx': nrm(ks[5], (D,), 1.0),
        'norm_g': 1.0 + nrm(ks[6], (DEPTH, 2, D), 0.02),
        'ada_w': nrm(ks[7], (DEPTH, D, 6 * D), D ** -0.5),
        'ada_b': nrm(ks[8], (DEPTH, 6 * D), 0.02),
        'a_w_in': nrm(ks[9], (N_A, D, 2 * A_HALF), D ** -0.5),
        'a_v_gain': 1.0 + nrm(ks[10], (N_A, A_HALF), 0.02),
        'a_ws': nrm(ks[11], (N_A, A_GROUPS, CHUNK, CHUNK), CHUNK ** -0.5),
        'a_bs': 1.0 + nrm(ks[12], (N_A, A_GROUPS, CHUNK), 0.02),
        'a_w_out': nrm(ks[13], (N_A, A_HALF, D), A_HALF ** -0.5),
        'b_w_qkv': nrm(ks[14], (N_B, D, 3 * D), D ** -0.5),
        'b_q_gain': 1.0 + nrm(ks[15], (N_B, HEAD_DIM), 0.02),
        'b_k_gain': 1.0 + nrm(ks[16], (N_B, HEAD_DIM), 0.02),
        'b_rpb': nrm(ks[17], (N_B, N_HEADS, 2 * WIN_ROWS_MAX - 1, 2 * WIN_COLS - 1), 0.1),
        'b_w_o': nrm(ks[18], (N_B, D, D), D ** -0.5),
        'c_w_in': nrm(ks[19], (N_C, D, 3 * D), D ** -0.5),
        'c_conv_w': nrm(ks[20], (N_C, CONV_W, D), CONV_W ** -0.5),
        'c_conv_b': nrm(ks[21], (N_C, D), 0.02),
        'c_w_out': nrm(ks[22], (N_C, D, D), D ** -0.5),
        'ff_w1': nrm(ks[23], (DEPTH, D, D_FF), D ** -0.5),
        'ff_w2': nrm(ks[24], (DEPTH, D_FF, D), D_FF ** -0.5),
    }


def reference(x_prompt, x_sample, cache_k, cache_v, c, c_ctx, norm_g, ada_w, ada_b,
              a_w_in, a_v_gain, a_ws, a_bs, a_w_out,
              b_w_qkv, b_q_gain, b_k_gain, b_rpb, b_w_o,
              c_w_in, c_conv_w, c_conv_b, c_w_out, ff_w1, ff_w2):
    xp, xs = x_prompt, x_sample
    cp = jnp.broadcast_to(c_ctx, (xp.shape[0], c_ctx.shape[0]))
    new_k, new_v = [], []
    for i in range(DEPTH):
        kind, j = i % N_MIX, i // N_MIX
        mp = modulation(cp, ada_w[i], ada_b[i])
        ms = modulation(c, ada_w[i], ada_b[i])
        hp = modulate(rms_norm(xp, norm_g[i, 0]), mp[0], mp[1])
        hs = modulate(rms_norm(xs, norm_g[i, 0]), ms[0], ms[1])
        if kind == 0:
            op = chunk_gmlp(hp, a_w_in[j], a_v_gain[j], a_ws[j], a_bs[j], a_w_out[j])
            os_ = chunk_gmlp(hs, a_w_in[j], a_v_gain[j], a_ws[j], a_bs[j], a_w_out[j])
        elif kind == 1:
            qp, kp, vp = qkv_heads(hp, b_w_qkv[j], b_q_gain[j], b_k_gain[j])
            new_k.append(kp)
            new_v.append(vp)
            op = context_attention(qp, kp, vp) @ b_w_o[j]
            qs, ks_, vs = qkv_heads(hs, b_w_qkv[j], b_q_gain[j], b_k_gain[j])
            os_ = neighbourhood_attention(qs, ks_, vs, cache_k[:, j], cache_v[:, j], b_rpb[j]) @ b_w_o[j]
        else:
            op = short_gated_conv(hp, c_w_in[j], c_conv_w[j], c_conv_b[j], c_w_out[j])
            os_ = short_gated_conv(hs, c_w_in[j], c_conv_w[j], c_conv_b[j], c_w_out[j])
        xp = xp + mp[2] * op
        xs = xs + ms[2] * os_
        hp = modulate(rms_norm(xp, norm_g[i, 1]), mp[3], mp[4])
        hs = modulate(rms_norm(xs, norm_g[i, 1]), ms[3], ms[4])
        xp = xp + mp[5] * sq_relu_mlp(hp, ff_w1[i], ff_w2[i])
        xs = xs + ms[5] * sq_relu_mlp(hs, ff_w1[i], ff_w2[i])
    new_cache_k = jnp.stack(new_k, axis=1)
    new_cache_v = jnp.stack(new_v, axis=1)
    return (xp, xs, new_cache_k, new_cache_v)
```

```python
import numpy as np
from contextlib import ExitStack
import concourse.bass as bass
import concourse.mybir as mybir
from concourse.bass_utils import run_bass_kernel_spmd

F32 = mybir.dt.float32
BF16 = mybir.dt.bfloat16
AF = mybir.ActivationFunctionType
ALU = mybir.AluOpType
ENGS = ["pe", "act", "dve", "pool", "sp"]
NCORES = 8
D = 1024
T = 1536
EPS = 1e-6
NSLOT = 3
ARENA_BYTES = 90 * 1024
FILL = -30000.0


class Res:
    __slots__ = ("w", "r", "excl")

    def __init__(self, excl=False):
        self.w = None
        self.r = {}
        self.excl = excl


class Op:
    __slots__ = ("eng", "fn", "deps", "sig", "sigval", "is_dma", "dsem", "dval")


class Prog:
    def __init__(self):
        self.ops = {e: [] for e in ENGS}
        self.dma_count = {}
        self.last = {e: None for e in ENGS}
        self.dma_ops = []

    def add(self, eng, fn, reads=(), writes=(), deps=(), dma_key=None):
        dl = [d for d in deps if d is not None]
        for R in reads:
            if R.w is not None:
                dl.append(R.w)
            if R.excl:
                for k_, v in R.r.items():
                    if k_ != eng and not isinstance(v, list):
                        dl.append(v)
        for R in writes:
            if R.w is not None:
                dl.append(R.w)
            for v in R.r.values():
                if isinstance(v, list):
                    dl.extend(v)
                else:
                    dl.append(v)
        if eng == "pe":
            dl = [d for d in dl if d.is_dma or d.eng != "pe"]
        op = Op()
        op.eng = eng
        op.fn = fn
        op.deps = dl
        op.sig = False
        op.sigval = 0
        op.is_dma = dma_key is not None
        op.dsem = dma_key
        op.dval = 0
        if op.is_dma:
            self.dma_count[dma_key] = self.dma_count.get(dma_key, 0) + 1
            op.dval = 16 * self.dma_count[dma_key]
            self.dma_ops.append(op)
        for d in dl:
            if not d.is_dma:
                d.sig = True
        for R in reads:
            if op.is_dma:
                R.r.setdefault("dma", []).append(op)
            else:
                R.r[eng] = op
        for R in writes:
            R.w = op
            R.r = {}
        self.ops[eng].append(op)
        if not op.is_dma:
            self.last[eng] = op
        return op

    def emit(self, nc, stack):
        for e in ENGS:
            c = 0
            for op in self.ops[e]:
                if op.sig:
                    c += 1
                    op.sigval = c
        esem = {e: stack.enter_context(nc.semaphore("s_" + e)) for e in ENGS}
        dsem = {k: stack.enter_context(nc.semaphore("d_" + str(k))) for k in self.dma_count}
        block = stack.enter_context(nc.Block())
        prog = self

        def run(ename, eng):
            waited = {}
            for op in prog.ops[ename]:
                for d in op.deps:
                    if d.is_dma:
                        key = ("d", d.dsem)
                        val = d.dval
                        sem = dsem[d.dsem]
                    else:
                        key = ("e", d.eng)
                        val = d.sigval
                        sem = esem[d.eng]
                    if waited.get(key, 0) < val:
                        eng.wait_ge(sem, val)
                        waited[key] = val
                if op.fn is None:
                    continue
                ins = op.fn(eng)
                if op.is_dma:
                    ins.then_inc(dsem[op.dsem], 16)
                elif op.sig:
                    ins.then_inc(esem[ename], 1)

        @block.tensor
        def _(e):
            run("pe", e)

        @block.scalar
        def _(e):
            run("act", e)

        @block.vector
        def _(e):
            run("dve", e)

        @block.gpsimd
        def _(e):
            run("pool", e)

        @block.sync
        def _(e):
            run("sp", e)


def bc_last(ap, n):
    return bass.AP(ap.tensor, ap.offset, [list(x) for x in ap.ap] + [[0, n]])


def bc_mid(ap, n):
    a = [list(x) for x in ap.ap]
    return bass.AP(ap.tensor, ap.offset, [a[0], [0, n]] + a[1:])


R_COND = 0
R_ADAB = 16
R_NORMG = 208
R_VGAIN = 272
R_CONVW = 304
R_CONVB = 328
R_QG = 336
R_KG = 337
NVEC = 384


def build(LAYERS=((0, 0, 0), (1, 1, 0), (2, 2, 0), (3, 0, 1))):
    nc = bass.Bass("TRN2", target_bir_lowering=False)

    def din(name, shape):
        return nc.dram_tensor(name, shape, F32, kind="ExternalInput")

    def dout(name, shape):
        return nc.dram_tensor(name, shape, F32, kind="ExternalOutput")

    xs_d = din("xs", [1024, D]).ap()
    xp_d = din("xp", [512, D]).ap()
    ck_d = din("ck", [512, D]).ap()
    cv_d = din("cv", [512, D]).ap()
    vecs_d = din("vecs", [NVEC, 128]).ap()
    abs_d = din("a_bs", [2, 1024])
    aws_d = din("a_ws", [2, 8, 128, 128]).ap()
    rp_d = din("rp", [16, 17, 127])
    cmask_d = din("cmask", [128, 64]).ap()
    ident_d = din("ident", [128, 128]).ap()
    ada_w = din("ada_w", [4, D, 6 * D]).ap()
    a_w_in = din("a_w_in", [2, D, 4096]).ap()
    a_w_out = din("a_w_out", [2, 2048, D]).ap()
    b_w_qkv = din("b_w_qkv", [D, 3 * D]).ap()
    b_w_o = din("b_w_o", [D, D]).ap()
    c_w_in = din("c_w_in", [D, 3 * D]).ap()
    c_w_out = din("c_w_out", [D, D]).ap()
    ff_w1 = din("ff_w1", [4, D, 4096]).ap()
    ff_w2 = din("ff_w2", [4, 4096, D]).ap()
    ys_d = dout("ys", [1024, D]).ap()
    yp_d = dout("yp", [512, D]).ap()
    nk_d = dout("nk", [512, D]).ap()
    nv_d = dout("nv", [512, D]).ap()

    P = Prog()
    st = ExitStack()
    with st:
        def sb(name, shape, dtype):
            return st.enter_context(nc.sbuf_tensor(name, shape, dtype))

        xres = sb("xres", [128, 8, T], F32)
        hT = sb("hT", [128, 8, T], BF16)
        ring = [sb("ring%d" % i, [128, 4096], BF16) for i in range(NSLOT)]
        arena = sb("arena", [128, ARENA_BYTES // 2], BF16)
        vecT = sb("vecT", [128, NVEC], F32)
        sc = sb("sc", [128, 8, 2], BF16)
        modT = [sb("modT%d" % i, [128, 48, 2], F32) for i in range(4)]
        gsT = [sb("gsT%d" % i, [128, 2, 8, 2], F32) for i in range(4)]
        ident_f = sb("ident_f", [128, 128], F32)
        ones_m = sb("ones_m", [128, 128], BF16)
        blk64 = sb("blk64", [128, 128], BF16)
        ones64 = sb("ones64", [128, 64], BF16)
        epsc = sb("epsc", [128, 1], F32)
        sqb = [sb("sqb%d" % i, [128, 8, 512], BF16) for i in range(1)]
        rstd = [sb("rstd%d" % i, [128, 512], F32) for i in range(2)]
        ntmp = [sb("ntmp%d" % i, [128, 512], F32) for i in range(2)]
        PS = [st.enter_context(nc.psum_tensor("ps%d" % i, [128, 512], F32)) for i in range(8)]
        PSR = [Res(excl=True) for _ in range(8)]

        XR = [[Res() for _ in range(3)] for _ in range(8)]
        HT = [[Res() for _ in range(3)] for _ in range(8)]
        SLOT = [Res() for _ in range(NSLOT)]
        SQ = [Res() for _ in range(1)]
        RSTD = [Res() for _ in range(2)]
        NTMP = [Res() for _ in range(2)]
        R_vecT = Res()
        R_sc = Res()
        R_const = Res()
        R_mod = [Res() for _ in range(4)]
        R_gs = [Res() for _ in range(4)]

        state = {"bank": 0, "reserved": set(), "sq": 0, "nt": 0, "ev": 0, "job": 0}

        def next_bank():
            while True:
                b = state["bank"] % 8
                state["bank"] += 1
                if b not in state["reserved"]:
                    return b

        ar = {"off": 0, "fence": {}}

        def arena_reset():
            ar["off"] = 0
            f = {}
            for e in ["pe", "act", "dve", "pool"]:
                if P.last[e] is not None:
                    f[e] = P.last[e]
            f["dma"] = [o for o in P.dma_ops if not str(o.dsem).startswith("w")]
            ar["fence"] = f

        def new_res():
            r = Res()
            r.r = dict(ar["fence"])
            if "dma" in r.r:
                r.r["dma"] = list(r.r["dma"])
            return r

        def aalloc(dtype, free_shape):
            n = 1
            for s in free_shape:
                n *= s
            nb = n * (4 if dtype == F32 else 2)
            off = (ar["off"] + 63) // 64 * 64
            assert off + nb <= ARENA_BYTES, ("arena overflow", off, nb)
            ar["off"] = off + nb
            ap = arena[:, off // 2:(off + nb) // 2]
            if dtype == F32:
                ap = ap.bitcast(F32)
            if len(free_shape) == 2:
                ap = ap.rearrange("p (a b) -> p a b", a=free_shape[0])
            elif len(free_shape) == 3:
                ap = ap.rearrange("p (a b c) -> p a b c", a=free_shape[0], b=free_shape[1])
            return ap

        def wload(src_ap, view):
            s = state["job"] % NSLOT
            state["job"] += 1
            srcs = src_ap if isinstance(src_ap, list) else [src_ap]
            dsts = view(ring[s])
            dsts = dsts if isinstance(dsts, list) else [dsts]
            first = None
            last = None
            for dst, src in zip(dsts, srcs):
                if first is None:
                    first = P.add("pool", lambda e, dst=dst, src=src: e.dma_start(out=dst, in_=src),
                                  writes=[SLOT[s]], dma_key="w%d" % s)
                    last = first
                else:
                    last = P.add("pool", lambda e, dst=dst, src=src: e.dma_start(out=dst, in_=src),
                                 deps=list(first.deps), dma_key="w%d" % s)
            SLOT[s].w = last
            return s

        def wview(kc, n):
            return lambda t: t[:, 0:kc * n].rearrange("p (k n) -> p k n", k=kc)

        def wsrc(w2d, r0, kc, c0, n):
            return w2d[r0:r0 + kc * 128, c0:c0 + n].rearrange("(k p) n -> p k n", p=128)

        def tile_sl(tt):
            return slice(tt * 512, (tt + 1) * 512)

        def evac_eng():
            state["ev"] += 1
            return "act" if state["ev"] % 2 else "dve"

        d_id = P.add("sp", lambda e: e.dma_start(out=ident_f[:], in_=ident_d), writes=[R_const], dma_key="ident")
        P.add("pool", lambda e: e.memset(ones_m[:], 1.0 / 1024.0), writes=[R_const])
        P.add("pool", lambda e: e.memset(blk64[:], 0.0), writes=[R_const])
        P.add("pool", lambda e: e.memset(blk64[0:64, 0:64], 1.0 / 64.0), writes=[R_const])
        P.add("pool", lambda e: e.memset(blk64[64:128, 64:128], 1.0 / 64.0), writes=[R_const])
        P.add("pool", lambda e: e.memset(ones64[:], 1.0), writes=[R_const])
        P.add("pool", lambda e: e.memset(epsc[:], EPS), writes=[R_const])
        arena_reset()
        vst = aalloc(F32, [3, 128])
        R_vst = new_res()
        P.add("sp", lambda e: e.dma_start(out=vst, in_=vecs_d.rearrange("(t p) f -> p t f", p=128)),
              writes=[R_vst], dma_key="vecs")
        b = next_bank()
        for t3 in range(3):
            P.add("pe", lambda e, t3=t3, b=b: e.transpose(out=PS[b][:, t3 * 128:(t3 + 1) * 128], in_=vst[:, t3, :], identity=ident_f[:]),
                  reads=[R_vst, R_const], writes=[PSR[b]])
        P.add("dve", lambda e, b=b: e.tensor_copy(out=vecT[:], in_=PS[b][:, 0:NVEC]), reads=[PSR[b]], writes=[R_vecT])
        P.add("act", lambda e: e.activation(out=sc[:].rearrange("p k j -> p (k j)"), in_=vecT[:, 0:16], func=AF.Silu),
              reads=[R_vecT], writes=[R_sc])

        arena_reset()
        xst = [aalloc(F32, [1024]) for _ in range(4)]
        XST = [new_res() for _ in range(4)]

        def load_x(tile_hook=None):
            for c in range(12):
                src = xs_d[c * 128:(c + 1) * 128, :] if c < 8 else xp_d[(c - 8) * 128:(c - 7) * 128, :]
                k = c % 4
                P.add("sp", lambda e, k=k, src=src: e.dma_start(out=xst[k], in_=src), writes=[XST[k]], dma_key="xst%d" % k)
                tt = c // 4
                for g in range(2):
                    b = next_bank()
                    for j in range(4):
                        fc = g * 4 + j
                        P.add("pe", lambda e, k=k, fc=fc, b=b, j=j: e.transpose(
                            out=PS[b][:, j * 128:(j + 1) * 128], in_=xst[k][:, fc * 128:(fc + 1) * 128], identity=ident_f[:]),
                            reads=[XST[k], R_const], writes=[PSR[b]])
                    eng = evac_eng()
                    outap = xres[:, g * 4:(g + 1) * 4, c * 128:(c + 1) * 128]
                    inap = PS[b][:].rearrange("p (j t) -> p j t", j=4)
                    if eng == "act":
                        P.add("act", lambda e, outap=outap, inap=inap: e.activation(out=outap, in_=inap, func=AF.Copy),
                              reads=[PSR[b]], writes=[XR[g * 4 + j][tt] for j in range(4)])
                    else:
                        P.add("dve", lambda e, outap=outap, inap=inap: e.tensor_copy(out=outap, in_=inap),
                              reads=[PSR[b]], writes=[XR[g * 4 + j][tt] for j in range(4)])
                if c % 4 == 3 and tile_hook is not None:
                    tile_hook(tt)

        def modulation(i, jobs=range(12)):
            if "modbank" not in state:
                b = next_bank()
                state["modbank"] = b
                state["reserved"].add(b)
            b = state["modbank"]
            for j in jobs:
                s = wload(wsrc(ada_w[i], 0, 8, j * 512, 512), wview(8, 512))
                for ocl in range(4):
                    oc = j * 4 + ocl
                    for kc in range(8):
                        P.add("pe", lambda e, s=s, kc=kc, ocl=ocl, oc=oc, b=b: e.matmul(
                            PS[b][:, oc * 2:oc * 2 + 2], lhsT=ring[s][:, kc * 512 + ocl * 128:kc * 512 + (ocl + 1) * 128],
                            rhs=sc[:, kc, :], start=(kc == 0), stop=(kc == 7)),
                            reads=[SLOT[s], R_sc], writes=[PSR[b]])
                if j in (3, 5, 11):
                    c0, c1, n = {3: (0, 16, 0), 5: (16, 24, None), 11: (24, 48, 1)}[j]
                    P.add("dve", lambda e, i=i, b=b, c0=c0, c1=c1: e.tensor_tensor(
                        out=modT[i][:, c0:c1, :], in0=PS[b][:, 2 * c0:2 * c1].rearrange("p (c j) -> p c j", j=2),
                        in1=bc_last(vecT[:, R_ADAB + i * 48 + c0:R_ADAB + i * 48 + c1], 2), op=ALU.add),
                        reads=[PSR[b], R_vecT], writes=[R_mod[i]])
                    if n is not None:
                        P.add("dve", lambda e, i=i, n=n: e.scalar_tensor_tensor(
                            out=gsT[i][:, n, :, :], in0=modT[i][:, 8 + 24 * n:16 + 24 * n, :], scalar=1.0,
                            in1=bc_last(vecT[:, R_NORMG + (i * 2 + n) * 8:R_NORMG + (i * 2 + n + 1) * 8], 2),
                            op0=ALU.add, op1=ALU.mult), reads=[R_mod[i], R_vecT], writes=[R_gs[i]])
                    if j == 11:
                        state["reserved"].discard(b)
                        del state["modbank"]

        def nm_A(i, n, tt):
            P.add("act", lambda e, tsl=tile_sl(tt): e.activation(out=sqb[0][:], in_=xres[:, :, tsl], func=AF.Square),
                  reads=[XR[kc][tt] for kc in range(8)], writes=[SQ[0]])

        def nm_P(i, n, tt):
            b = next_bank()
            for kc in range(8):
                P.add("pe", lambda e, kc=kc, b=b: e.matmul(PS[b][:], lhsT=ones_m[:], rhs=sqb[0][:, kc, :],
                                                         start=(kc == 0), stop=(kc == 7)),
                      reads=[SQ[0], R_const], writes=[PSR[b]])
            return b

        def nm_B(i, n, tt, b):
            j = 0 if tt < 2 else 1
            kr = state["sq"] % 2
            state["sq"] += 1
            tsl = tile_sl(tt)
            P.add("act", lambda e, kr=kr, b=b: e.activation(out=rstd[kr][:], in_=PS[b][:], func=AF.Ln, bias=epsc[:, 0:1], scale=1.0),
                  reads=[PSR[b], R_const], writes=[RSTD[kr]])
            P.add("act", lambda e, kr=kr: e.activation(out=rstd[kr][:], in_=rstd[kr][:], func=AF.Exp, scale=-0.5), reads=[RSTD[kr]], writes=[RSTD[kr]])
            for kc in range(8):
                m = state["nt"] % 2
                state["nt"] += 1
                P.add("dve", lambda e, kc=kc, m=m, kr=kr, tsl=tsl: e.scalar_tensor_tensor(
                    out=ntmp[m][:], in0=xres[:, kc, tsl], scalar=gsT[i][:, n, kc, j:j + 1], in1=rstd[kr][:],
                    op0=ALU.mult, op1=ALU.mult), reads=[XR[kc][tt], RSTD[kr], R_gs[i]], writes=[NTMP[m]])
                if kc % 4 != 3:
                    P.add("act", lambda e, kc=kc, m=m, tsl=tsl: e.activation(
                        out=hT[:, kc, tsl], in_=ntmp[m][:], func=AF.Identity, bias=modT[i][:, 24 * n + kc, j:j + 1], scale=1.0),
                        reads=[NTMP[m], R_mod[i]], writes=[HT[kc][tt]])
                else:
                    P.add("dve", lambda e, kc=kc, m=m, tsl=tsl: e.tensor_scalar(
                        out=hT[:, kc, tsl], in0=ntmp[m][:], scalar1=modT[i][:, 24 * n + kc, j:j + 1], scalar2=None, op0=ALU.add),
                        reads=[NTMP[m], R_mod[i]], writes=[HT[kc][tt]])

        def norm_mod(i, n, tt):
            nm_A(i, n, tt)
            b = nm_P(i, n, tt)
            nm_B(i, n, tt, b)

        class NormSched:
            def __init__(self, i, n):
                self.i, self.n, self.prev = i, n, None

            def tile_done(self, tt):
                b = nm_P(self.i, self.n, self.prev) if self.prev is not None else None
                nm_A(self.i, self.n, tt)
                if self.prev is not None:
                    nm_B(self.i, self.n, self.prev, b)
                self.prev = tt

            def flush(self):
                if self.prev is not None:
                    b = nm_P(self.i, self.n, self.prev)
                    nm_B(self.i, self.n, self.prev, b)
                    self.prev = None

        def resid_evac(b, i, n, oc, tt):
            j = 0 if tt < 2 else 1
            tsl = tile_sl(tt)
            P.add("dve", lambda e, b=b, oc=oc, tsl=tsl: e.scalar_tensor_tensor(
                out=xres[:, oc, tsl], in0=PS[b][:], scalar=modT[i][:, 16 + 24 * n + oc, j:j + 1], in1=xres[:, oc, tsl],
                op0=ALU.mult, op1=ALU.add), reads=[PSR[b], R_mod[i], XR[oc][tt]], writes=[XR[oc][tt]])

        def out_proj(i, n, jobs, rhs_fn, rhs_res_fn, nk, on_tile_done=None, between=None):
            oc0 = 0

            def grp(s, ncols, oc_base, ocl, tt):
                b = next_bank()
                for kc in range(nk):
                    P.add("pe", lambda e, kc=kc, b=b: e.matmul(
                        PS[b][:], lhsT=ring[s][:, kc * ncols + ocl * 128:kc * ncols + (ocl + 1) * 128],
                        rhs=rhs_fn(kc, tt), start=(kc == 0), stop=(kc == nk - 1)),
                        reads=[SLOT[s], rhs_res_fn(kc, tt)], writes=[PSR[b]])
                resid_evac(b, i, n, oc_base + ocl, tt)

            ntail = min(2, len(jobs)) if on_tile_done is not None else 0
            for (loader, ncols) in jobs[:len(jobs) - ntail]:
                s = loader()
                for ocl in range(ncols // 128):
                    for tt in range(3):
                        grp(s, ncols, oc0, ocl, tt)
                oc0 += ncols // 128
                if between is not None:
                    between()
            if ntail:
                tail = []
                for (loader, ncols) in jobs[len(jobs) - ntail:]:
                    tail.append((loader(), ncols, oc0))
                    oc0 += ncols // 128
                    if between is not None:
                        between()
                for tt in range(3):
                    for (s, ncols, ocb) in tail:
                        for ocl in range(ncols // 128):
                            grp(s, ncols, ocb, ocl, tt)
                    on_tile_done(tt)

        def ffn(i, between=None, on_tile_done=None, early_hook=None):
            for half in range(2):
                arena_reset()
                hid = aalloc(BF16, [16, T])
                HID = [[new_res() for _ in range(3)] for _ in range(16)]
                rt = [aalloc(BF16, [512]) for _ in range(3)]
                RT = [new_res() for _ in range(3)]
                rk = [0]
                def grp1(s, j, ocl, tt):
                    hc = j * 4 + ocl
                    b = next_bank()
                    for kc in range(8):
                        P.add("pe", lambda e, kc=kc, b=b: e.matmul(
                            PS[b][:], lhsT=ring[s][:, kc * 512 + ocl * 128:kc * 512 + (ocl + 1) * 128],
                            rhs=hT[:, kc, tile_sl(tt)], start=(kc == 0), stop=(kc == 7)),
                            reads=[SLOT[s], HT[kc][tt]], writes=[PSR[b]])
                    m = rk[0] % 3
                    rk[0] += 1
                    P.add("act", lambda e, m=m, b=b: e.activation(out=rt[m], in_=PS[b][:], func=AF.Relu),
                          reads=[PSR[b]], writes=[RT[m]])
                    P.add("dve", lambda e, m=m: e.tensor_tensor(
                        out=hid[:, hc, tile_sl(tt)], in0=rt[m], in1=rt[m], op=ALU.mult),
                        reads=[RT[m]], writes=[HID[hc][tt]])

                def w1job(j):
                    return wload(wsrc(ff_w1[i], 0, 8, half * 2048 + j * 512, 512), wview(8, 512))
                j = 0
                if half == 0:
                    sa, sb = w1job(0), w1job(1)
                    for tt in range(3):
                        for (s_, j_) in ((sa, 0), (sb, 1)):
                            for ocl in range(4):
                                grp1(s_, j_, ocl, tt)
                        if tt == 0 and early_hook is not None:
                            early_hook()
                    j = 2
                while j < 4:
                    s = w1job(j)
                    for ocl in range(4):
                        for tt in range(3):
                            grp1(s, j, ocl, tt)
                    if between is not None:
                        between()
                    j += 1
                jobs = [((lambda j=j: wload(wsrc(ff_w2[i], half * 2048, 16, j * 256, 256), wview(16, 256))), 256) for j in range(4)]
                out_proj(i, 1, jobs, lambda kc, tt: hid[:, kc, tile_sl(tt)], lambda kc, tt: HID[kc][tt], 16,
                         on_tile_done=(on_tile_done if half == 1 else None), between=between)

        def gmlp(i, jl, on_tile_done=None, early_hook=None, between=None):
            arena_reset()
            wsn = aalloc(F32, [8, 128])
            wsT = aalloc(BF16, [8, 128])
            bsb = aalloc(F32, [8, 128])
            R_wsn, R_wsT, R_bsb = new_res(), new_res(), new_res()
            P.add("sp", lambda e: e.dma_start(out=wsn, in_=aws_d[jl].rearrange("g p q -> p g q")), writes=[R_wsn], dma_key="wsn")
            P.add("sp", lambda e: e.dma_start(out=bsb, in_=bass.AP(abs_d, jl * 1024, [[0, 128], [128, 8], [1, 128]])),
                  writes=[R_bsb], dma_key="bsb")
            for g2 in range(2):
                b = next_bank()
                for g in range(4):
                    P.add("pe", lambda e, g=g, g2=g2, b=b: e.transpose(out=PS[b][:, g * 128:(g + 1) * 128], in_=wsn[:, g2 * 4 + g, :], identity=ident_f[:]),
                          reads=[R_wsn, R_const], writes=[PSR[b]])
                P.add("dve", lambda e, g2=g2, b=b: e.tensor_copy(out=wsT[:, g2 * 4:(g2 + 1) * 4, :], in_=PS[b][:].rearrange("p (g q) -> p g q", g=4)),
                      reads=[PSR[b]], writes=[R_wsT])
            vtok = aalloc(BF16, [12, 2048])
            VT = [[new_res() for _ in range(4)] for _ in range(12)]
            ush = aalloc(BF16, [8, T])
            usb = [wsn[:, 4 * k_:4 * k_ + 4, :].rearrange("p a q -> p (a q)") for k_ in range(2)]
            ssb = [aalloc(F32, [4, 128]) for _ in range(2)]
            USB = [new_res() for _ in range(2)]
            SSB = [new_res() for _ in range(2)]
            ss = aalloc(F32, [12, 4])
            sst = aalloc(F32, [12])
            junk = ssb[0].rearrange("p a q -> p (a q)").bitcast(BF16)[:, 0:512]
            R_ss, R_sst, R_junk = new_res(), new_res(), new_res()
            for jv in range(4):
                s = wload(wsrc(a_w_in[jl], 0, 8, 2048 + jv * 512, 512), wview(8, 512))
                for ch in range(12):
                    b = next_bank()
                    tok = slice(ch * 128, (ch + 1) * 128)
                    for kc in range(8):
                        P.add("pe", lambda e, s=s, kc=kc, tok=tok, b=b: e.matmul(
                            PS[b][:], lhsT=hT[:, kc, tok], rhs=ring[s][:, kc * 512:(kc + 1) * 512],
                            start=(kc == 0), stop=(kc == 7)), reads=[SLOT[s], HT[kc][ch // 4]], writes=[PSR[b]])
                    P.add("act", lambda e, ch=ch, jv=jv, b=b: e.activation(
                        out=vtok[:, ch, jv * 512:(jv + 1) * 512], in_=PS[b][:], func=AF.Gelu_apprx_tanh),
                        reads=[PSR[b]], writes=[VT[ch][jv]])
                    P.add("dve", lambda e, ch=ch, jv=jv: e.scalar_tensor_tensor(
                        out=junk, in0=vtok[:, ch, jv * 512:(jv + 1) * 512], scalar=1.0, in1=vtok[:, ch, jv * 512:(jv + 1) * 512],
                        op0=ALU.mult, op1=ALU.mult, accum_out=ss[:, ch, jv:jv + 1]),
                        reads=[VT[ch][jv]], writes=[R_junk, R_ss])
                    if jv == 0 and ch == 3 and early_hook is not None:
                        early_hook()
                if between is not None:
                    between()
            P.add("dve", lambda e: e.reduce_sum(out=sst, in_=ss, axis=mybir.AxisListType.X), reads=[R_ss], writes=[R_sst])
            P.add("act", lambda e: e.activation(out=sst, in_=sst, func=AF.Ln, bias=epsc[:, 0:1], scale=1.0 / 2048.0),
                  reads=[R_sst, R_const], writes=[R_sst])
            P.add("act", lambda e: e.activation(out=sst, in_=sst, func=AF.Exp, scale=-0.5), reads=[R_sst], writes=[R_sst])
            for ch in range(12):
                P.add("dve", lambda e, ch=ch: e.tensor_scalar(
                    out=vtok[:, ch, :], in0=vtok[:, ch, :], scalar1=sst[:, ch:ch + 1], scalar2=None, op0=ALU.mult),
                    reads=[R_sst] + VT[ch], writes=VT[ch])
            for hf in range(2):
                US = [[new_res() for _ in range(3)] for _ in range(8)]
                for ju in range(2 * hf, 2 * hf + 2):
                    if between is not None:
                        between()
                    s = wload(wsrc(a_w_in[jl], 0, 8, ju * 512, 512), wview(8, 512))
                    for cl in range(4):
                        c = ju * 4 + cl
                        g = c // 2
                        vg = vecT[:, R_VGAIN + jl * 16 + c:R_VGAIN + jl * 16 + c + 1]
                        for tt in range(3):
                            bs_ = next_bank()
                            for c4 in range(4):
                                ch = tt * 4 + c4
                                P.add("pe", lambda e, c=c, c4=c4, ch=ch, g=g, bs_=bs_: e.matmul(
                                    PS[bs_][:, c4 * 128:(c4 + 1) * 128], lhsT=vtok[:, ch, c * 128:(c + 1) * 128],
                                    rhs=wsT[:, g, :], start=True, stop=True),
                                    reads=[VT[ch][c // 4], R_wsT], writes=[PSR[bs_]])
                            bu = next_bank()
                            for kc in range(8):
                                P.add("pe", lambda e, s=s, kc=kc, cl=cl, bu=bu, tt=tt: e.matmul(
                                    PS[bu][:], lhsT=ring[s][:, kc * 512 + cl * 128:kc * 512 + (cl + 1) * 128],
                                    rhs=hT[:, kc, tile_sl(tt)], start=(kc == 0), stop=(kc == 7)),
                                    reads=[SLOT[s], HT[kc][tt]], writes=[PSR[bu]])
                            m = state["ev"] % 2
                            state["ev"] += 1
                            P.add("dve", lambda e, m=m, bs_=bs_, g=g, vg=vg: e.scalar_tensor_tensor(
                                out=ssb[m], in0=PS[bs_][:].rearrange("p (a q) -> p a q", a=4), scalar=vg,
                                in1=bc_mid(bsb[:, g, :], 4), op0=ALU.mult, op1=ALU.add),
                                reads=[PSR[bs_], R_vecT, R_bsb], writes=[SSB[m]])
                            P.add("act", lambda e, m=m, bu=bu: e.activation(out=usb[m], in_=PS[bu][:], func=AF.Gelu_apprx_tanh),
                                  reads=[PSR[bu]], writes=[USB[m]])
                            P.add("dve", lambda e, m=m, c=c, tt=tt, hf=hf: e.tensor_tensor(
                                out=ush[:, c - 8 * hf, tile_sl(tt)], in0=usb[m], in1=ssb[m].rearrange("p a q -> p (a q)"), op=ALU.mult),
                                reads=[USB[m], SSB[m]], writes=[US[c - 8 * hf][tt]])
                jobs = [((lambda jo=jo, hf=hf: wload(wsrc(a_w_out[jl], hf * 1024, 8, jo * 512, 512), wview(8, 512))), 512) for jo in range(2)]
                out_proj(i, 0, jobs, lambda kc, tt: ush[:, kc, tile_sl(tt)], lambda kc, tt, US=US: US[kc][tt], 8,
                         on_tile_done=(on_tile_done if hf == 1 else None))

        def conv(i, jl, on_tile_done=None, early_hook=None):
            if early_hook is not None:
                early_hook()
            arena_reset()
            gT = aalloc(BF16, [8, T])
            GT = [[new_res() for _ in range(3)] for _ in range(8)]
            xtb = aalloc(F32, [T])
            zb = aalloc(F32, [T])
            accb = aalloc(F32, [T])
            bgb = aalloc(F32, [T])
            R_bg = [new_res() for _ in range(3)]
            R_xt = [new_res() for _ in range(3)]
            R_z = [new_res() for _ in range(3)]
            R_acc = [new_res() for _ in range(3)]
            w3 = c_w_in.rearrange("(k p) (b n) -> p k b n", p=128, b=3)
            seqs = [(0, 1024), (1024, 256), (1280, 256)]

            def tiles_of(s0, ln):
                return sorted(set([s0 // 512, (s0 + ln - 1) // 512]))

            for fc in range(8):
                s = wload([w3[:, :, b_, fc * 128:(fc + 1) * 128] for b_ in range(3)],
                          lambda t: [t[:, 0:8 * 384].rearrange("p (k b n) -> p k b n", k=8, b=3)[:, :, b_, :] for b_ in range(3)])

                def proj(bsel, tt, s=s):
                    b = next_bank()
                    for kc in range(8):
                        P.add("pe", lambda e, kc=kc, b=b, bsel=bsel, tt=tt: e.matmul(
                            PS[b][:], lhsT=ring[s][:, kc * 384 + bsel * 128:kc * 384 + (bsel + 1) * 128],
                            rhs=hT[:, kc, tile_sl(tt)], start=(kc == 0), stop=(kc == 7)),
                            reads=[SLOT[s], HT[kc][tt]], writes=[PSR[b]])
                    return b
                for tt in range(3):
                    b = proj(2, tt)
                    P.add("act", lambda e, b=b, tt=tt: e.activation(out=xtb[:, tile_sl(tt)], in_=PS[b][:], func=AF.Copy),
                          reads=[PSR[b]], writes=[R_xt[tt]])
                for tt in range(3):
                    b = proj(1, tt)
                    P.add("dve", lambda e, b=b, tt=tt: e.tensor_tensor(out=zb[:, tile_sl(tt)], in0=PS[b][:], in1=xtb[:, tile_sl(tt)], op=ALU.mult),
                          reads=[PSR[b], R_xt[tt]], writes=[R_z[tt]])
                w0 = vecT[:, R_CONVW + 0 * 8 + fc:R_CONVW + 0 * 8 + fc + 1]
                w1 = vecT[:, R_CONVW + 1 * 8 + fc:R_CONVW + 1 * 8 + fc + 1]
                w2 = vecT[:, R_CONVW + 2 * 8 + fc:R_CONVW + 2 * 8 + fc + 1]
                cb = vecT[:, R_CONVB + fc:R_CONVB + fc + 1]
                for tt in range(3):
                    P.add("act", lambda e, tt=tt, w1=w1, cb=cb: e.activation(out=accb[:, tile_sl(tt)], in_=zb[:, tile_sl(tt)], func=AF.Identity, bias=cb, scale=w1),
                          reads=[R_z[tt], R_vecT], writes=[R_acc[tt]])
                for (s0, ln) in seqs:
                    tl = tiles_of(s0, ln)
                    P.add("dve", lambda e, s0=s0, ln=ln, w0=w0: e.scalar_tensor_tensor(
                        out=accb[:, s0 + 1:s0 + ln], in0=zb[:, s0:s0 + ln - 1], scalar=w0, in1=accb[:, s0 + 1:s0 + ln],
                        op0=ALU.mult, op1=ALU.add), reads=[R_z[t_] for t_ in tl] + [R_vecT], writes=[R_acc[t_] for t_ in tl])
                    P.add("dve", lambda e, s0=s0, ln=ln, w2=w2: e.scalar_tensor_tensor(
                        out=accb[:, s0:s0 + ln - 1], in0=zb[:, s0 + 1:s0 + ln], scalar=w2, in1=accb[:, s0:s0 + ln - 1],
                        op0=ALU.mult, op1=ALU.add), reads=[R_z[t_] for t_ in tl] + [R_vecT], writes=[R_acc[t_] for t_ in tl])
                for tt in range(3):
                    b = proj(0, tt)
                    P.add("dve", lambda e, b=b, tt=tt, fc=fc: e.tensor_tensor(out=gT[:, fc, tile_sl(tt)], in0=PS[b][:], in1=accb[:, tile_sl(tt)], op=ALU.mult),
                          reads=[PSR[b], R_acc[tt]], writes=[GT[fc][tt]])
            jobs = [((lambda j=j: wload(wsrc(c_w_out, 0, 8, j * 512, 512), wview(8, 512))), 512) for j in range(2)]
            out_proj(i, 0, jobs, lambda kc, tt: gT[:, kc, tile_sl(tt)], lambda kc, tt: GT[kc][tt], 8, on_tile_done=on_tile_done)

        def attention(i, jl, on_tile_done=None, early_hook=None):
            if early_hook is not None:
                early_hook()
            arena_reset()
            attnT = aalloc(BF16, [8, T])
            AT = [[new_res() for _ in range(3)] for _ in range(8)]
            qT = [aalloc(BF16, [T]) for _ in range(2)]
            kT = [aalloc(BF16, [T]) for _ in range(2)]
            vtk = [aalloc(BF16, [12, 128]) for _ in range(2)]
            kcT = [aalloc(BF16, [512]) for _ in range(2)]
            vc = [aalloc(BF16, [4, 128]) for _ in range(2)]
            ckst = [aalloc(F32, [4, 128]) for _ in range(2)]
            cvst = [aalloc(F32, [4, 128]) for _ in range(2)]
            eraw = [aalloc(F32, [16, 64]) for _ in range(2)]
            ET = [[aalloc(BF16, [16, 64]) for _ in range(2)] for _ in range(2)]
            cmk = aalloc(F32, [64])
            Pb = [aalloc(BF16, [512]) for _ in range(6)]
            sqn = [aalloc(BF16, [512]) for _ in range(2)]
            rsn1 = aalloc(F32, [512])
            rsn = [rsn1, rsn1]
            kn32 = aalloc(F32, [512])
            nkst = aalloc(F32, [4, 128])
            nvst = aalloc(F32, [4, 128])
            rden = [aalloc(F32, [512]) for _ in range(1)]
            gqs = aalloc(F32, [1])
            R_cmk, R_gqs = new_res(), new_res()
            R_QT = [[new_res() for _ in range(3)] for _ in range(2)]
            R_KT = [[new_res() for _ in range(3)] for _ in range(2)]
            R_VTK = [[new_res() for _ in range(3)] for _ in range(2)]
            R_kcT = [new_res() for _ in range(2)]
            R_vc = [new_res() for _ in range(2)]
            R_ckst = [new_res() for _ in range(2)]
            R_cvst = [new_res() for _ in range(2)]
            R_eraw = [new_res() for _ in range(2)]
            R_ET = [[new_res() for _ in range(2)] for _ in range(2)]
            R_Pb = [new_res() for _ in range(6)]
            R_sqn = [new_res() for _ in range(2)]
            R_rsn1 = new_res()
            R_rsn = [R_rsn1, R_rsn1]
            R_kn32, R_nkst, R_nvst = new_res(), new_res(), new_res()
            R_rden = [new_res() for _ in range(1)]
            cnt = {"pb": 0, "nrm": 0, "et": 0, "rd": 0}
            P.add("sp", lambda e: e.dma_start(out=cmk, in_=cmask_d), writes=[R_cmk], dma_key="cmk")
            P.add("dve", lambda e: e.tensor_scalar(out=gqs, in0=vecT[:, R_QG:R_QG + 1], scalar1=0.125, scalar2=None, op0=ALU.mult),
                  reads=[R_vecT], writes=[R_gqs])
            gk = vecT[:, R_KG:R_KG + 1]
            w3 = b_w_qkv.rearrange("(k p) (b n) -> p k b n", p=128, b=3)
            nk_v = nk_d.rearrange("(c p) f -> p c f", p=128)
            nv_v = nv_d.rearrange("(c p) f -> p c f", p=128)
            out_dmas = []

            def qrange(kt):
                if kt <= 3:
                    return 0, 2 * kt + 5, (0, 2 * kt + 5)
                return 2 * kt - 3, 15, (1, 2 * kt - 3)

            wslot = {}

            def wreq(hp):
                if hp < 8 and hp not in wslot:
                    wslot[hp] = wload([w3[:, :, b_, hp * 128:(hp + 1) * 128] for b_ in range(3)],
                                      lambda t: [t[:, 0:8 * 384].rearrange("p (k b n) -> p k b n", k=8, b=3)[:, :, b_, :] for b_ in range(3)])

            def setup(hp):
                pp = hp % 2
                wreq(hp)
                wreq(hp + 1)
                s = wslot[hp]
                for a_h in range(2):
                    h = hp * 2 + a_h
                    for aa in range(2):
                        t0 = 1 - aa
                        src = bass.AP(rp_d, h * 17 * 127 + t0 * 127, [[1, 64], [127, 16], [1, 64]])
                        P.add("sp", lambda e, a_h=a_h, aa=aa, src=src: e.dma_start(out=eraw[a_h][aa * 64:(aa + 1) * 64, :, :], in_=src),
                              writes=[R_eraw[a_h]], dma_key="eraw%d" % a_h)
                P.add("sp", lambda e, pp=pp, hp=hp: e.dma_start(out=ckst[pp], in_=ck_d.rearrange("(t p) f -> p t f", p=128)[:, :, hp * 128:(hp + 1) * 128]),
                      writes=[R_ckst[pp]], dma_key="ckst%d" % pp)
                P.add("sp", lambda e, pp=pp, hp=hp: e.dma_start(out=cvst[pp], in_=cv_d.rearrange("(t p) f -> p t f", p=128)[:, :, hp * 128:(hp + 1) * 128]),
                      writes=[R_cvst[pp]], dma_key="cvst%d" % pp)
                yield
                for bsel in range(2):
                    for tt in range(3):
                        b = next_bank()
                        for kc in range(8):
                            P.add("pe", lambda e, kc=kc, b=b, bsel=bsel, tt=tt, s=s: e.matmul(
                                PS[b][:], lhsT=ring[s][:, kc * 384 + bsel * 128:kc * 384 + (bsel + 1) * 128],
                                rhs=hT[:, kc, tile_sl(tt)], start=(kc == 0), stop=(kc == 7)),
                                reads=[SLOT[s], HT[kc][tt]], writes=[PSR[b]])
                        m = cnt["nrm"] % 2
                        cnt["nrm"] += 1
                        P.add("act", lambda e, m=m, b=b: e.activation(out=sqn[m], in_=PS[b][:], func=AF.Square), reads=[PSR[b]], writes=[R_sqn[m]])
                        b2 = next_bank()
                        P.add("pe", lambda e, m=m, b2=b2: e.matmul(PS[b2][:], lhsT=blk64[:], rhs=sqn[m], start=True, stop=True),
                              reads=[R_sqn[m], R_const], writes=[PSR[b2]])
                        P.add("act", lambda e, m=m, b2=b2: e.activation(out=rsn[m], in_=PS[b2][:], func=AF.Ln, bias=epsc[:, 0:1], scale=1.0),
                              reads=[PSR[b2], R_const], writes=[R_rsn[m]])
                        P.add("act", lambda e, m=m: e.activation(out=rsn[m], in_=rsn[m], func=AF.Exp, scale=-0.5), reads=[R_rsn[m]], writes=[R_rsn[m]])
                        tsl = tile_sl(tt)
                        if bsel == 0:
                            P.add("dve", lambda e, m=m, b=b, pp=pp, tsl=tsl: e.scalar_tensor_tensor(
                                out=qT[pp][:, tsl], in0=PS[b][:], scalar=gqs[:, 0:1], in1=rsn[m], op0=ALU.mult, op1=ALU.mult),
                                reads=[PSR[b], R_rsn[m], R_gqs], writes=[R_QT[pp][tt]])
                        elif tt < 2:
                            P.add("dve", lambda e, m=m, b=b, pp=pp, tsl=tsl: e.scalar_tensor_tensor(
                                out=kT[pp][:, tsl], in0=PS[b][:], scalar=gk, in1=rsn[m], op0=ALU.mult, op1=ALU.mult),
                                reads=[PSR[b], R_rsn[m], R_vecT], writes=[R_KT[pp][tt]])
                        else:
                            P.add("dve", lambda e, m=m, b=b: e.scalar_tensor_tensor(
                                out=kn32, in0=PS[b][:], scalar=gk, in1=rsn[m], op0=ALU.mult, op1=ALU.mult),
                                reads=[PSR[b], R_rsn[m], R_vecT], writes=[R_kn32])
                            P.add("act", lambda e, pp=pp, tsl=tsl: e.activation(out=kT[pp][:, tsl], in_=kn32, func=AF.Copy),
                                  reads=[R_kn32], writes=[R_KT[pp][tt]])
                            b3 = next_bank()
                            for c4 in range(4):
                                P.add("pe", lambda e, c4=c4, b3=b3: e.transpose(out=PS[b3][:, c4 * 128:(c4 + 1) * 128], in_=kn32[:, c4 * 128:(c4 + 1) * 128], identity=ident_f[:]),
                                      reads=[R_kn32, R_const], writes=[PSR[b3]])
                            P.add("dve", lambda e, b3=b3: e.tensor_copy(out=nkst, in_=PS[b3][:].rearrange("p (c f) -> p c f", c=4)),
                                  reads=[PSR[b3]], writes=[R_nkst])
                            out_dmas.append(P.add("sp", lambda e, hp=hp: e.dma_start(out=nk_v[:, :, hp * 128:(hp + 1) * 128], in_=nkst),
                                                  reads=[R_nkst], dma_key="nkst"))
                        yield
                b = next_bank()
                for t4 in range(4):
                    P.add("pe", lambda e, pp=pp, t4=t4, b=b: e.transpose(out=PS[b][:, t4 * 128:(t4 + 1) * 128], in_=ckst[pp][:, t4, :], identity=ident_f[:]),
                          reads=[R_ckst[pp], R_const], writes=[PSR[b]])
                P.add("dve", lambda e, pp=pp, b=b: e.tensor_copy(out=kcT[pp], in_=PS[b][:]), reads=[PSR[b]], writes=[R_kcT[pp]])
                P.add("pool", lambda e, pp=pp: e.tensor_copy(out=vc[pp], in_=cvst[pp]), reads=[R_cvst[pp]], writes=[R_vc[pp]])
                yield
                for c3 in range(3):
                    b = next_bank()
                    for c4 in range(4):
                        c = c3 * 4 + c4
                        for kc in range(8):
                            P.add("pe", lambda e, kc=kc, b=b, c=c, c4=c4, s=s: e.matmul(
                                PS[b][:, c4 * 128:(c4 + 1) * 128], lhsT=hT[:, kc, c * 128:(c + 1) * 128],
                                rhs=ring[s][:, kc * 384 + 256:kc * 384 + 384], start=(kc == 0), stop=(kc == 7)),
                                reads=[SLOT[s], HT[kc][c3]], writes=[PSR[b]])
                    P.add("dve", lambda e, pp=pp, c3=c3, b=b: e.tensor_copy(out=vtk[pp][:, c3 * 4:(c3 + 1) * 4, :], in_=PS[b][:].rearrange("p (c f) -> p c f", c=4)),
                          reads=[PSR[b]], writes=[R_VTK[pp][c3]])
                    if c3 == 2:
                        P.add("dve", lambda e, b=b: e.tensor_copy(out=nvst, in_=PS[b][:].rearrange("p (c f) -> p c f", c=4)),
                              reads=[PSR[b]], writes=[R_nvst])
                        out_dmas.append(P.add("sp", lambda e, hp=hp: e.dma_start(out=nv_v[:, :, hp * 128:(hp + 1) * 128], in_=nvst),
                                              reads=[R_nvst], dma_key="nvst"))
                    yield
                for a_h in range(2):
                    P.add("act", lambda e, a_h=a_h: e.activation(out=eraw[a_h], in_=eraw[a_h], func=AF.Exp), reads=[R_eraw[a_h]], writes=[R_eraw[a_h]])
                    P.add("dve", lambda e, a_h=a_h, pp=pp: e.tensor_tensor(out=ET[pp][a_h], in0=eraw[a_h], in1=bc_mid(cmk, 16), op=ALU.mult),
                          reads=[R_eraw[a_h], R_cmk], writes=[R_ET[pp][a_h]])
                    yield

            def core(hp, gen):
                pp = hp % 2
                tiles = []
                groups = []

                def add_group(cols, tt_list):
                    g_ = {"bn": None, "bd": None, "cols": cols, "tt": tt_list}
                    groups.append(g_)
                    return g_

                for qt in range(2):
                    g_ = add_group(slice(qt * 512, (qt + 1) * 512), [qt])
                    for kt in range(4):
                        for a_h in range(2):
                            hb = a_h * 64
                            tiles.append(dict(g=g_, lhsT=kcT[pp][hb:hb + 64, kt * 128:(kt + 1) * 128], rhs=qT[pp][hb:hb + 64, qt * 512:(qt + 1) * 512],
                                              n=512, reads=[R_kcT[pp], R_QT[pp][qt]], et=None, zero=None, hb=hb, c0=0,
                                              vlhs=vc[pp][:, kt, hb:hb + 64], vreads=[R_vc[pp]], first=(kt == 0)))
                    for kt in range(8):
                        qlo, qhi, (ih, irow) = qrange(kt)
                        r0 = max(qlo, 8 * qt)
                        r1 = min(qhi, 8 * qt + 7)
                        if r1 < r0:
                            continue
                        nr = r1 - r0 + 1
                        j0 = r0 - 2 * kt + 7
                        for a_h in range(2):
                            hb = a_h * 64
                            v_ = pp
                            ea = ET[pp][a_h]
                            e_in = bass.AP(ea.tensor, ea.offset + j0 * 64 + 63, [list(ea.ap[0]), [64, nr], [-1, 64]])
                            zero = (ih, (irow - r0) * 64) if r0 <= irow <= r1 else None
                            tiles.append(dict(g=g_, lhsT=kT[pp][hb:hb + 64, kt * 128:(kt + 1) * 128], rhs=qT[pp][hb:hb + 64, r0 * 64:(r1 + 1) * 64],
                                              n=nr * 64, reads=[R_KT[pp][kt // 4], R_QT[pp][qt]], et=(e_in, nr, a_h, v_), zero=zero, hb=hb,
                                              c0=(r0 - 8 * qt) * 64, vlhs=vtk[pp][:, kt, hb:hb + 64], vreads=[R_VTK[pp][kt // 4]], first=False))
                g_ = add_group(slice(1024, 1536), [2])
                for sq_ in range(2):
                    t0 = 1024 + 256 * sq_
                    for kt in range(2):
                        for a_h in range(2):
                            hb = a_h * 64
                            tiles.append(dict(g=g_, lhsT=kT[pp][hb:hb + 64, t0 + kt * 128:t0 + (kt + 1) * 128], rhs=qT[pp][hb:hb + 64, t0:t0 + 256],
                                              n=256, reads=[R_KT[pp][2], R_QT[pp][2]], et=None, zero=None, hb=hb, c0=sq_ * 256,
                                              vlhs=vtk[pp][:, 8 + 2 * sq_ + kt, hb:hb + 64], vreads=[R_VTK[pp][2]], first=(sq_ == 0 and kt == 0)))
                for idx_, t_ in enumerate(tiles):
                    t_["last"] = (idx_ + 1 == len(tiles)) or (tiles[idx_ + 1]["g"] is not t_["g"])
                LAGP = 2
                npair = len(tiles) // 2
                assert len(tiles) % 2 == 0
                for step in range(npair + LAGP):
                    if step % 2 == 1:
                        next(gen, None)
                    if step < npair:
                        pend = []
                        for t_ in (tiles[2 * step], tiles[2 * step + 1]):
                            g_ = t_["g"]
                            if g_["bn"] is None:
                                g_["bn"] = next_bank()
                                state["reserved"].add(g_["bn"])
                                g_["bd"] = next_bank()
                                state["reserved"].add(g_["bd"])
                            n = t_["n"]
                            bsx = next_bank()
                            P.add("pe", lambda e, bsx=bsx, t_=t_, n=n: e.matmul(PS[bsx][:, 0:n], lhsT=t_["lhsT"], rhs=t_["rhs"], start=True, stop=True),
                                  reads=t_["reads"], writes=[PSR[bsx]])
                            m = cnt["pb"] % 6
                            cnt["pb"] += 1
                            t_["m"] = m
                            pend.append((t_, bsx, m, n))
                        for (t_, bsx, m, n) in pend:
                            P.add("act", lambda e, m=m, bsx=bsx, n=n: e.activation(out=Pb[m][:, 0:n], in_=PS[bsx][:, 0:n], func=AF.Exp),
                                  reads=[PSR[bsx]], writes=[R_Pb[m]])
                            if t_["et"] is not None:
                                e_in, nr, a_h, v_ = t_["et"]
                                P.add("dve", lambda e, m=m, n=n, nr=nr, e_in=e_in: e.tensor_tensor(
                                    out=Pb[m][:, 0:n].rearrange("p (r c) -> p r c", r=nr), in0=Pb[m][:, 0:n].rearrange("p (r c) -> p r c", r=nr),
                                    in1=e_in, op=ALU.mult), reads=[R_Pb[m], R_ET[v_][a_h]], writes=[R_Pb[m]])
                            if t_["zero"] is not None:
                                ih, zc = t_["zero"]
                                P.add("dve", lambda e, m=m, ih=ih, zc=zc: e.memset(Pb[m][ih * 64:(ih + 1) * 64, zc:zc + 64], 0.0), writes=[R_Pb[m]])
                    j_ = step - LAGP
                    if j_ >= 0:
                        pair = (tiles[2 * j_], tiles[2 * j_ + 1])
                        for t_ in pair:
                            g_ = t_["g"]
                            m, n, hb, c0, bn = t_["m"], t_["n"], t_["hb"], t_["c0"], g_["bn"]
                            P.add("pe", lambda e, m=m, n=n, bn=bn, hb=hb, c0=c0, t_=t_: e.matmul(
                                PS[bn][hb:hb + 64, c0:c0 + n], lhsT=t_["vlhs"], rhs=Pb[m][:, 0:n], start=t_["first"], stop=False, skip_group_check=True),
                                reads=[R_Pb[m]] + t_["vreads"], writes=[PSR[bn]])
                        for t_ in pair:
                            g_ = t_["g"]
                            m, n, hb, c0, bd = t_["m"], t_["n"], t_["hb"], t_["c0"], g_["bd"]
                            P.add("pe", lambda e, m=m, n=n, bd=bd, hb=hb, c0=c0, t_=t_: e.matmul(
                                PS[bd][hb:hb + 64, c0:c0 + n], lhsT=ones64[:], rhs=Pb[m][:, 0:n], start=t_["first"], stop=False, skip_group_check=True),
                                reads=[R_Pb[m], R_const], writes=[PSR[bd]])
                        t_ = pair[1]
                        g_ = t_["g"]
                        bn, bd = g_["bn"], g_["bd"]
                        if t_["last"]:
                            cols = g_["cols"]
                            mr = 0
                            cnt["rd"] += 1
                            nn = cols.stop - cols.start
                            P.add("act", lambda e, mr=mr, bd=bd, nn=nn: e.activation(out=rden[mr][:, 0:nn], in_=PS[bd][:, 0:nn], func=AF.Ln), reads=[PSR[bd]], writes=[R_rden[mr]])
                            P.add("act", lambda e, mr=mr, nn=nn: e.activation(out=rden[mr][:, 0:nn], in_=rden[mr][:, 0:nn], func=AF.Exp, scale=-1.0), reads=[R_rden[mr]], writes=[R_rden[mr]])
                            P.add("dve", lambda e, mr=mr, bn=bn, nn=nn, cols=cols, hp=hp: e.tensor_tensor(
                                out=attnT[:, hp, cols], in0=PS[bn][:, 0:nn], in1=rden[mr][:, 0:nn], op=ALU.mult),
                                reads=[PSR[bn], R_rden[mr]], writes=[AT[hp][t2_] for t2_ in g_["tt"]])
                            state["reserved"].discard(bn)
                            state["reserved"].discard(bd)
            for _ in setup(0):
                pass
            for hp in range(8):
                gen = setup(hp + 1) if hp + 1 < 8 else iter(())
                core(hp, gen)
                for _ in gen:
                    pass
            jobs = [((lambda j=j: wload(wsrc(b_w_o, 0, 8, j * 512, 512), wview(8, 512))), 512) for j in range(2)]
            out_proj(i, 0, jobs, lambda kc, tt: attnT[:, kc, tile_sl(tt)], lambda kc, tt: AT[kc][tt], 8, on_tile_done=on_tile_done)
            return out_dmas

        ost = [arena[:, (ARENA_BYTES - 8192 + 4096 * k_) // 2:(ARENA_BYTES - 4096 + 4096 * k_) // 2].bitcast(F32) for k_ in range(2)]
        OST = [Res() for _ in range(2)]
        outs = []

        def out_tile(tt):
            for c in range(tt * 4, tt * 4 + 4):
                k = c % 2
                for g in range(2):
                    b = next_bank()
                    for j in range(4):
                        fc = g * 4 + j
                        P.add("pe", lambda e, c=c, fc=fc, b=b, j=j: e.transpose(
                            out=PS[b][:, j * 128:(j + 1) * 128], in_=xres[:, fc, c * 128:(c + 1) * 128], identity=ident_f[:]),
                            reads=[XR[fc][tt], R_const], writes=[PSR[b]])
                    eng = evac_eng()
                    outap = ost[k][:, g * 512:(g + 1) * 512]
                    if eng == "act":
                        P.add("act", lambda e, outap=outap, b=b: e.activation(out=outap, in_=PS[b][:], func=AF.Copy),
                              reads=[PSR[b]], writes=[OST[k]])
                    else:
                        P.add("dve", lambda e, outap=outap, b=b: e.tensor_copy(out=outap, in_=PS[b][:]),
                              reads=[PSR[b]], writes=[OST[k]])
                dst = ys_d[c * 128:(c + 1) * 128, :] if c < 8 else yp_d[(c - 8) * 128:(c - 7) * 128, :]
                outs.append(P.add("sp", lambda e, k=k, dst=dst: e.dma_start(out=dst, in_=ost[k]), reads=[OST[k]], dma_key="ost%d" % k))

        class OutSched:
            def __init__(self):
                self.prev = None

            def tile_done(self, tt):
                if self.prev is not None:
                    out_tile(self.prev)
                self.prev = tt

            def flush(self):
                if self.prev is not None:
                    out_tile(self.prev)
                    self.prev = None

        extra_outs = []
        first_pending = list(range(12))
        i0 = LAYERS[0][0]
        if LAYERS[0][1] == 0:
            modulation(i0, first_pending[:4])
            first_pending = first_pending[4:]
        else:
            modulation(i0, first_pending)
            first_pending = []

        def first_between():
            if first_pending:
                modulation(i0, [first_pending.pop(0)])
        sched0 = NormSched(i0, 0)
        load_x(sched0.tile_done)
        sched1 = sched0
        for li, (i, kind, jl) in enumerate(LAYERS):
            sched2 = NormSched(i, 1)
            eh = sched1.flush if sched1 is not None else None
            if kind == 0:
                gmlp(i, jl, on_tile_done=sched2.tile_done, early_hook=eh, between=(first_between if li == 0 else None))
                assert li != 0 or not first_pending
            elif kind == 1:
                extra_outs += attention(i, jl, on_tile_done=sched2.tile_done, early_hook=eh)
            else:
                conv(i, jl, on_tile_done=sched2.tile_done, early_hook=eh)
            if li + 1 < len(LAYERS):
                pending = list(range(12))
                nxt = LAYERS[li + 1][0]

                def between(nxt=nxt, pending=pending):
                    if pending:
                        j = pending.pop(0)
                        modulation(nxt, [j])
                sched1 = NormSched(nxt, 0)
                ffn(i, between, on_tile_done=sched1.tile_done, early_hook=sched2.flush)
                assert not pending
            else:
                sched1 = None
                osched = OutSched()
                ffn(i, None, on_tile_done=osched.tile_done, early_hook=sched2.flush)
                osched.flush()

        P.add("sp", None, deps=outs + extra_outs)
        P.emit(nc, st)
    return nc


def make_in_maps(inp):
    f = lambda a: np.ascontiguousarray(np.asarray(a, dtype=np.float32))
    x_prompt, x_sample = f(inp["x_prompt"]), f(inp["x_sample"])
    cache_k, cache_v = f(inp["cache_k"]), f(inp["cache_v"])
    c, c_ctx = f(inp["c"]), f(inp["c_ctx"])
    shared = {
        "a_bs": f(inp["a_bs"]).reshape(2, 1024),
        "a_ws": f(inp["a_ws"]),
        "ident": np.eye(128, dtype=np.float32),
        "ada_w": f(inp["ada_w"]), "a_w_in": f(inp["a_w_in"]), "a_w_out": f(inp["a_w_out"]),
        "b_w_qkv": f(inp["b_w_qkv"])[0], "b_w_o": f(inp["b_w_o"])[0],
        "c_w_in": f(inp["c_w_in"])[0], "c_w_out": f(inp["c_w_out"])[0],
        "ff_w1": f(inp["ff_w1"]), "ff_w2": f(inp["ff_w2"]),
    }
    rpb = f(inp["b_rpb"])[0]
    rp = np.full((16, 17, 127), FILL, np.float32)
    rp[:, 1:16, 48:79] = rpb[:, ::-1, :]
    shared["rp"] = rp
    col = np.arange(64)
    cstart = np.clip(col - 8, 0, 48)
    ok = (col[None, :] >= cstart[:, None]) & (col[None, :] < cstart[:, None] + 16)
    m = ok.T[:, ::-1].astype(np.float32)
    shared["cmask"] = np.ascontiguousarray(np.concatenate([m, m], axis=0))
    vec_common = np.zeros((NVEC, 128), np.float32)
    vec_common[R_ADAB:R_ADAB + 192] = f(inp["ada_b"]).reshape(192, 128)
    vec_common[R_NORMG:R_NORMG + 64] = f(inp["norm_g"]).reshape(64, 128)
    vec_common[R_VGAIN:R_VGAIN + 32] = f(inp["a_v_gain"]).reshape(32, 128)
    vec_common[R_CONVW:R_CONVW + 24] = f(inp["c_conv_w"]).reshape(24, 128)
    vec_common[R_CONVB:R_CONVB + 8] = f(inp["c_conv_b"]).reshape(8, 128)
    vec_common[R_QG] = np.tile(f(inp["b_q_gain"])[0], 2)
    vec_common[R_KG] = np.tile(f(inp["b_k_gain"])[0], 2)
    maps = []
    for i in range(NCORES):
        v = vec_common.copy()
        cond = np.stack([c[i].reshape(8, 128), c_ctx.reshape(8, 128)], axis=1).reshape(16, 128)
        v[R_COND:R_COND + 16] = cond
        m_ = dict(shared)
        m_["xs"] = x_sample[i]
        m_["xp"] = x_prompt[2 * i:2 * i + 2].reshape(512, D)
        m_["ck"] = cache_k[i, 0].reshape(512, D)
        m_["cv"] = cache_v[i, 0].reshape(512, D)
        m_["vecs"] = v
        maps.append(m_)
    return maps


_NC_CACHE = {}


FULL = ((0, 0, 0), (1, 1, 0), (2, 2, 0), (3, 0, 1))


def run(inputs, LAYERS=FULL, trace=False):
    LAYERS = tuple(tuple(x) for x in LAYERS)
    if LAYERS not in _NC_CACHE:
        _NC_CACHE[LAYERS] = build(LAYERS)
    nc = _NC_CACHE[LAYERS]
    maps = make_in_maps(inputs)
    res = run_bass_kernel_spmd(nc, maps, core_ids=list(range(NCORES)), trace=trace)
    r = res.results
    ys = np.stack([r[i]["ys"] for i in range(NCORES)], axis=0)
    yp = np.concatenate([r[i]["yp"].reshape(2, 256, D) for i in range(NCORES)], axis=0)
    nk = np.concatenate([r[i]["nk"].reshape(2, 1, 256, 16, 64) for i in range(NCORES)], axis=0)
    nv = np.concatenate([r[i]["nv"].reshape(2, 1, 256, 16, 64) for i in range(NCORES)], axis=0)
    return (yp, ys, nk, nv), res


def kernel(**inputs):
    out, _ = run(inputs, FULL)
    return out
```

```python
import numpy as np
from contextlib import ExitStack
import concourse.bass as bass
import concourse.mybir as mybir
from concourse.bass_utils import run_bass_kernel_spmd

F32 = mybir.dt.float32
BF16 = mybir.dt.bfloat16
AF = mybir.ActivationFunctionType
ALU = mybir.AluOpType
ENGS = ["pe", "act", "dve", "pool", "sp"]
NCORES = 8
D = 1024
T = 1536
EPS = 1e-6
NSLOT = 3
ARENA_BYTES = 90 * 1024
FILL = -30000.0


class Res:
    __slots__ = ("w", "r", "excl")

    def __init__(self, excl=False):
        self.w = None
        self.r = {}
        self.excl = excl


class Op:
    __slots__ = ("eng", "fn", "deps", "sig", "sigval", "is_dma", "dsem", "dval")


class Prog:
    def __init__(self):
        self.ops = {e: [] for e in ENGS}
        self.dma_count = {}
        self.last = {e: None for e in ENGS}
        self.dma_ops = []

    def add(self, eng, fn, reads=(), writes=(), deps=(), dma_key=None):
        dl = [d for d in deps if d is not None]
        for R in reads:
            if R.w is not None:
                dl.append(R.w)
            if R.excl:
                for k_, v in R.r.items():
                    if k_ != eng and not isinstance(v, list):
                        dl.append(v)
        for R in writes:
            if R.w is not None:
                dl.append(R.w)
            for v in R.r.values():
                if isinstance(v, list):
                    dl.extend(v)
                else:
                    dl.append(v)
        if eng == "pe":
            dl = [d for d in dl if d.is_dma or d.eng != "pe"]
        op = Op()
        op.eng = eng
        op.fn = fn
        op.deps = dl
        op.sig = False
        op.sigval = 0
        op.is_dma = dma_key is not None
        op.dsem = dma_key
        op.dval = 0
        if op.is_dma:
            self.dma_count[dma_key] = self.dma_count.get(dma_key, 0) + 1
            op.dval = 16 * self.dma_count[dma_key]
            self.dma_ops.append(op)
        for d in dl:
            if not d.is_dma:
                d.sig = True
        for R in reads:
            if op.is_dma:
                R.r.setdefault("dma", []).append(op)
            else:
                R.r[eng] = op
        for R in writes:
            R.w = op
            R.r = {}
        self.ops[eng].append(op)
        if not op.is_dma:
            self.last[eng] = op
        return op

    def emit(self, nc, stack):
        for e in ENGS:
            c = 0
            for op in self.ops[e]:
                if op.sig:
                    c += 1
                    op.sigval = c
        esem = {e: stack.enter_context(nc.semaphore("s_" + e)) for e in ENGS}
        dsem = {k: stack.enter_context(nc.semaphore("d_" + str(k))) for k in self.dma_count}
        block = stack.enter_context(nc.Block())
        prog = self

        def run(ename, eng):
            waited = {}
            for op in prog.ops[ename]:
                for d in op.deps:
                    if d.is_dma:
                        key = ("d", d.dsem)
                        val = d.dval
                        sem = dsem[d.dsem]
                    else:
                        key = ("e", d.eng)
                        val = d.sigval
                        sem = esem[d.eng]
                    if waited.get(key, 0) < val:
                        eng.wait_ge(sem, val)
                        waited[key] = val
                if op.fn is None:
                    continue
                ins = op.fn(eng)
                if op.is_dma:
                    ins.then_inc(dsem[op.dsem], 16)
                elif op.sig:
                    ins.then_inc(esem[ename], 1)

        @block.tensor
        def _(e):
            run("pe", e)

        @block.scalar
        def _(e):
            run("act", e)

        @block.vector
        def _(e):
            run("dve", e)

        @block.gpsimd
        def _(e):
            run("pool", e)

        @block.sync
        def _(e):
            run("sp", e)


def bc_last(ap, n):
    return bass.AP(ap.tensor, ap.offset, [list(x) for x in ap.ap] + [[0, n]])


def bc_mid(ap, n):
    a = [list(x) for x in ap.ap]
    return bass.AP(ap.tensor, ap.offset, [a[0], [0, n]] + a[1:])


R_COND = 0
R_ADAB = 16
R_NORMG = 208
R_VGAIN = 272
R_CONVW = 304
R_CONVB = 328
R_QG = 336
R_KG = 337
NVEC = 384


def build(LAYERS=((0, 0, 0), (1, 1, 0), (2, 2, 0), (3, 0, 1))):
    nc = bass.Bass("TRN2", target_bir_lowering=False)

    def din(name, shape):
        return nc.dram_tensor(name, shape, F32, kind="ExternalInput")

    def dout(name, shape):
        return nc.dram_tensor(name, shape, F32, kind="ExternalOutput")

    xs_d = din("xs", [1024, D]).ap()
    xp_d = din("xp", [512, D]).ap()
    ck_d = din("ck", [512, D]).ap()
    cv_d = din("cv", [512, D]).ap()
    vecs_d = din("vecs", [NVEC, 128]).ap()
    abs_d = din("a_bs", [2, 1024])
    aws_d = din("a_ws", [2, 8, 128, 128]).ap()
    rp_d = din("rp", [16, 17, 127])
    cmask_d = din("cmask", [128, 64]).ap()
    ident_d = din("ident", [128, 128]).ap()
    ada_w = din("ada_w", [4, D, 6 * D]).ap()
    a_w_in = din("a_w_in", [2, D, 4096]).ap()
    a_w_out = din("a_w_out", [2, 2048, D]).ap()
    b_w_qkv = din("b_w_qkv", [D, 3 * D]).ap()
    b_w_o = din("b_w_o", [D, D]).ap()
    c_w_in = din("c_w_in", [D, 3 * D]).ap()
    c_w_out = din("c_w_out", [D, D]).ap()
    ff_w1 = din("ff_w1", [4, D, 4096]).ap()
    ff_w2 = din("ff_w2", [4, 4096, D]).ap()
    ys_d = dout("ys", [1024, D]).ap()
    yp_d = dout("yp", [512, D]).ap()
    nk_d = dout("nk", [512, D]).ap()
    nv_d = dout("nv", [512, D]).ap()

    P = Prog()
    st = ExitStack()
    with st:
        def sb(name, shape, dtype):
            return st.enter_context(nc.sbuf_tensor(name, shape, dtype))

        xres = sb("xres", [128, 8, T], F32)
        hT = sb("hT", [128, 8, T], BF16)
        ring = [sb("ring%d" % i, [128, 4096], BF16) for i in range(NSLOT)]
        arena = sb("arena", [128, ARENA_BYTES // 2], BF16)
        vecT = sb("vecT", [128, NVEC], F32)
        sc = sb("sc", [128, 8, 2], BF16)
        modT = [sb("modT%d" % i, [128, 48, 2], F32) for i in range(4)]
        gsT = [sb("gsT%d" % i, [128, 2, 8, 2], F32) for i in range(4)]
        ident_f = sb("ident_f", [128, 128], F32)
        ones_m = sb("ones_m", [128, 128], BF16)
        blk64 = sb("blk64", [128, 128], BF16)
        ones64 = sb("ones64", [128, 64], BF16)
        epsc = sb("epsc", [128, 1], F32)
        sqb = [sb("sqb%d" % i, [128, 8, 512], BF16) for i in range(1)]
        rstd = [sb("rstd%d" % i, [128, 512], F32) for i in range(2)]
        ntmp = [sb("ntmp%d" % i, [128, 512], F32) for i in range(2)]
        PS = [st.enter_context(nc.psum_tensor("ps%d" % i, [128, 512], F32)) for i in range(8)]
        PSR = [Res(excl=True) for _ in range(8)]

        XR = [[Res() for _ in range(3)] for _ in range(8)]
        HT = [[Res() for _ in range(3)] for _ in range(8)]
        SLOT = [Res() for _ in range(NSLOT)]
        SQ = [Res() for _ in range(1)]
        RSTD = [Res() for _ in range(2)]
        NTMP = [Res() for _ in range(2)]
        R_vecT = Res()
        R_sc = Res()
        R_const = Res()
        R_mod = [Res() for _ in range(4)]
        R_gs = [Res() for _ in range(4)]

        state = {"bank": 0, "reserved": set(), "sq": 0, "nt": 0, "ev": 0, "job": 0}

        def next_bank():
            while True:
                b = state["bank"] % 8
                state["bank"] += 1
                if b not in state["reserved"]:
                    return b

        ar = {"off": 0, "fence": {}}

        def arena_reset():
            ar["off"] = 0
            f = {}
            for e in ["pe", "act", "dve", "pool"]:
                if P.last[e] is not None:
                    f[e] = P.last[e]
            f["dma"] = [o for o in P.dma_ops if not str(o.dsem).startswith("w")]
            ar["fence"] = f

        def new_res():
            r = Res()
            r.r = dict(ar["fence"])
            if "dma" in r.r:
                r.r["dma"] = list(r.r["dma"])
            return r

        def aalloc(dtype, free_shape):
            n = 1
            for s in free_shape:
                n *= s
            nb = n * (4 if dtype == F32 else 2)
            off = (ar["off"] + 63) // 64 * 64
            assert off + nb <= ARENA_BYTES, ("arena overflow", off, nb)
            ar["off"] = off + nb
            ap = arena[:, off // 2:(off + nb) // 2]
            if dtype == F32:
                ap = ap.bitcast(F32)
            if len(free_shape) == 2:
                ap = ap.rearrange("p (a b) -> p a b", a=free_shape[0])
            elif len(free_shape) == 3:
                ap = ap.rearrange("p (a b c) -> p a b c", a=free_shape[0], b=free_shape[1])
            return ap

        def wload(src_ap, view):
            s = state["job"] % NSLOT
            state["job"] += 1
            srcs = src_ap if isinstance(src_ap, list) else [src_ap]
            dsts = view(ring[s])
            dsts = dsts if isinstance(dsts, list) else [dsts]
            first = None
            last = None
            for dst, src in zip(dsts, srcs):
                if first is None:
                    first = P.add("pool", lambda e, dst=dst, src=src: e.dma_start(out=dst, in_=src),
                                  writes=[SLOT[s]], dma_key="w%d" % s)
                    last = first
                else:
                    last = P.add("pool", lambda e, dst=dst, src=src: e.dma_start(out=dst, in_=src),
                                 deps=list(first.deps), dma_key="w%d" % s)
            SLOT[s].w = last
            return s

        def wview(kc, n):
            return lambda t: t[:, 0:kc * n].rearrange("p (k n) -> p k n", k=kc)

        def wsrc(w2d, r0, kc, c0, n):
            return w2d[r0:r0 + kc * 128, c0:c0 + n].rearrange("(k p) n -> p k n", p=128)

        def tile_sl(tt):
            return slice(tt * 512, (tt + 1) * 512)

        def evac_eng():
            state["ev"] += 1
            return "act" if state["ev"] % 2 else "dve"

        d_id = P.add("sp", lambda e: e.dma_start(out=ident_f[:], in_=ident_d), writes=[R_const], dma_key="ident")
        P.add("pool", lambda e: e.memset(ones_m[:], 1.0 / 1024.0), writes=[R_const])
        P.add("pool", lambda e: e.memset(blk64[:], 0.0), writes=[R_const])
        P.add("pool", lambda e: e.memset(blk64[0:64, 0:64], 1.0 / 64.0), writes=[R_const])
        P.add("pool", lambda e: e.memset(blk64[64:128, 64:128], 1.0 / 64.0), writes=[R_const])
        P.add("pool", lambda e: e.memset(ones64[:], 1.0), writes=[R_const])
        P.add("pool", lambda e: e.memset(epsc[:], EPS), writes=[R_const])
        arena_reset()
        vst = aalloc(F32, [3, 128])
        R_vst = new_res()
        P.add("sp", lambda e: e.dma_start(out=vst, in_=vecs_d.rearrange("(t p) f -> p t f", p=128)),
              writes=[R_vst], dma_key="vecs")
        b = next_bank()
        for t3 in range(3):
            P.add("pe", lambda e, t3=t3, b=b: e.transpose(out=PS[b][:, t3 * 128:(t3 + 1) * 128], in_=vst[:, t3, :], identity=ident_f[:]),
                  reads=[R_vst, R_const], writes=[PSR[b]])
        P.add("dve", lambda e, b=b: e.tensor_copy(out=vecT[:], in_=PS[b][:, 0:NVEC]), reads=[PSR[b]], writes=[R_vecT])
        P.add("act", lambda e: e.activation(out=sc[:].rearrange("p k j -> p (k j)"), in_=vecT[:, 0:16], func=AF.Silu),
              reads=[R_vecT], writes=[R_sc])

        arena_reset()
        xst = [aalloc(F32, [1024]) for _ in range(4)]
        XST = [new_res() for _ in range(4)]

        def load_x(tile_hook=None):
            for c in range(12):
                src = xs_d[c * 128:(c + 1) * 128, :] if c < 8 else xp_d[(c - 8) * 128:(c - 7) * 128, :]
                k = c % 4
                P.add("sp", lambda e, k=k, src=src: e.dma_start(out=xst[k], in_=src), writes=[XST[k]], dma_key="xst%d" % k)
                tt = c // 4
                for g in range(2):
                    b = next_bank()
                    for j in range(4):
                        fc = g * 4 + j
                        P.add("pe", lambda e, k=k, fc=fc, b=b, j=j: e.transpose(
                            out=PS[b][:, j * 128:(j + 1) * 128], in_=xst[k][:, fc * 128:(fc + 1) * 128], identity=ident_f[:]),
                            reads=[XST[k], R_const], writes=[PSR[b]])
                    eng = evac_eng()
                    outap = xres[:, g * 4:(g + 1) * 4, c * 128:(c + 1) * 128]
                    inap = PS[b][:].rearrange("p (j t) -> p j t", j=4)
                    if eng == "act":
                        P.add("act", lambda e, outap=outap, inap=inap: e.activation(out=outap, in_=inap, func=AF.Copy),
                              reads=[PSR[b]], writes=[XR[g * 4 + j][tt] for j in range(4)])
                    else:
                        P.add("dve", lambda e, outap=outap, inap=inap: e.tensor_copy(out=outap, in_=inap),
                              reads=[PSR[b]], writes=[XR[g * 4 + j][tt] for j in range(4)])
                if c % 4 == 3 and tile_hook is not None:
                    tile_hook(tt)

        def modulation(i, jobs=range(12)):
            if "modbank" not in state:
                b = next_bank()
                state["modbank"] = b
                state["reserved"].add(b)
            b = state["modbank"]
            for j in jobs:
                s = wload(wsrc(ada_w[i], 0, 8, j * 512, 512), wview(8, 512))
                for ocl in range(4):
                    oc = j * 4 + ocl
                    for kc in range(8):
                        P.add("pe", lambda e, s=s, kc=kc, ocl=ocl, oc=oc, b=b: e.matmul(
                            PS[b][:, oc * 2:oc * 2 + 2], lhsT=ring[s][:, kc * 512 + ocl * 128:kc * 512 + (ocl + 1) * 128],
                            rhs=sc[:, kc, :], start=(kc == 0), stop=(kc == 7)),
                            reads=[SLOT[s], R_sc], writes=[PSR[b]])
                if j in (3, 5, 11):
                    c0, c1, n = {3: (0, 16, 0), 5: (16, 24, None), 11: (24, 48, 1)}[j]
                    P.add("dve", lambda e, i=i, b=b, c0=c0, c1=c1: e.tensor_tensor(
                        out=modT[i][:, c0:c1, :], in0=PS[b][:, 2 * c0:2 * c1].rearrange("p (c j) -> p c j", j=2),
                        in1=bc_last(vecT[:, R_ADAB + i * 48 + c0:R_ADAB + i * 48 + c1], 2), op=ALU.add),
                        reads=[PSR[b], R_vecT], writes=[R_mod[i]])
                    if n is not None:
                        P.add("dve", lambda e, i=i, n=n: e.scalar_tensor_tensor(
                            out=gsT[i][:, n, :, :], in0=modT[i][:, 8 + 24 * n:16 + 24 * n, :], scalar=1.0,
                            in1=bc_last(vecT[:, R_NORMG + (i * 2 + n) * 8:R_NORMG + (i * 2 + n + 1) * 8], 2),
                            op0=ALU.add, op1=ALU.mult), reads=[R_mod[i], R_vecT], writes=[R_gs[i]])
                    if j == 11:
                        state["reserved"].discard(b)
                        del state["modbank"]

        def nm_A(i, n, tt):
            P.add("act", lambda e, tsl=tile_sl(tt): e.activation(out=sqb[0][:], in_=xres[:, :, tsl], func=AF.Square),
                  reads=[XR[kc][tt] for kc in range(8)], writes=[SQ[0]])

        def nm_P(i, n, tt):
            b = next_bank()
            for kc in range(8):
                P.add("pe", lambda e, kc=kc, b=b: e.matmul(PS[b][:], lhsT=ones_m[:], rhs=sqb[0][:, kc, :],
                                                         start=(kc == 0), stop=(kc == 7)),
                      reads=[SQ[0], R_const], writes=[PSR[b]])
            return b

        def nm_B(i, n, tt, b):
            j = 0 if tt < 2 else 1
            kr = state["sq"] % 2
            state["sq"] += 1
            tsl = tile_sl(tt)
            P.add("act", lambda e, kr=kr, b=b: e.activation(out=rstd[kr][:], in_=PS[b][:], func=AF.Ln, bias=epsc[:, 0:1], scale=1.0),
                  reads=[PSR[b], R_const], writes=[RSTD[kr]])
            P.add("act", lambda e, kr=kr: e.activation(out=rstd[kr][:], in_=rstd[kr][:], func=AF.Exp, scale=-0.5), reads=[RSTD[kr]], writes=[RSTD[kr]])
            for kc in range(8):
                m = state["nt"] % 2
                state["nt"] += 1
                P.add("dve", lambda e, kc=kc, m=m, kr=kr, tsl=tsl: e.scalar_tensor_tensor(
                    out=ntmp[m][:], in0=xres[:, kc, tsl], scalar=gsT[i][:, n, kc, j:j + 1], in1=rstd[kr][:],
                    op0=ALU.mult, op1=ALU.mult), reads=[XR[kc][tt], RSTD[kr], R_gs[i]], writes=[NTMP[m]])
                if kc % 4 != 3:
                    P.add("act", lambda e, kc=kc, m=m, tsl=tsl: e.activation(
                        out=hT[:, kc, tsl], in_=ntmp[m][:], func=AF.Identity, bias=modT[i][:, 24 * n + kc, j:j + 1], scale=1.0),
                        reads=[NTMP[m], R_mod[i]], writes=[HT[kc][tt]])
                else:
                    P.add("dve", lambda e, kc=kc, m=m, tsl=tsl: e.tensor_scalar(
                        out=hT[:, kc, tsl], in0=ntmp[m][:], scalar1=modT[i][:, 24 * n + kc, j:j + 1], scalar2=None, op0=ALU.add),
                        reads=[NTMP[m], R_mod[i]], writes=[HT[kc][tt]])

        def norm_mod(i, n, tt):
            nm_A(i, n, tt)
            b = nm_P(i, n, tt)
            nm_B(i, n, tt, b)

        class NormSched:
            def __init__(self, i, n):
                self.i, self.n, self.prev = i, n, None

            def tile_done(self, tt):
                b = nm_P(self.i, self.n, self.prev) if self.prev is not None else None
                nm_A(self.i, self.n, tt)
                if self.prev is not None:
                    nm_B(self.i, self.n, self.prev, b)
                self.prev = tt

            def flush(self):
                if self.prev is not None:
                    b = nm_P(self.i, self.n, self.prev)
                    nm_B(self.i, self.n, self.prev, b)
                    self.prev = None

        def resid_evac(b, i, n, oc, tt):
            j = 0 if tt < 2 else 1
            tsl = tile_sl(tt)
            P.add("dve", lambda e, b=b, oc=oc, tsl=tsl: e.scalar_tensor_tensor(
                out=xres[:, oc, tsl], in0=PS[b][:], scalar=modT[i][:, 16 + 24 * n + oc, j:j + 1], in1=xres[:, oc, tsl],
                op0=ALU.mult, op1=ALU.add), reads=[PSR[b], R_mod[i], XR[oc][tt]], writes=[XR[oc][tt]])

        def out_proj(i, n, jobs, rhs_fn, rhs_res_fn, nk, on_tile_done=None, between=None):
            oc0 = 0

            def grp(s, ncols, oc_base, ocl, tt):
                b = next_bank()
                for kc in range(nk):
                    P.add("pe", lambda e, kc=kc, b=b: e.matmul(
                        PS[b][:], lhsT=ring[s][:, kc * ncols + ocl * 128:kc * ncols + (ocl + 1) * 128],
                        rhs=rhs_fn(kc, tt), start=(kc == 0), stop=(kc == nk - 1)),
                        reads=[SLOT[s], rhs_res_fn(kc, tt)], writes=[PSR[b]])
                resid_evac(b, i, n, oc_base + ocl, tt)

            ntail = min(2, len(jobs)) if on_tile_done is not None else 0
            for (loader, ncols) in jobs[:len(jobs) - ntail]:
                s = loader()
                for ocl in range(ncols // 128):
                    for tt in range(3):
                        grp(s, ncols, oc0, ocl, tt)
                oc0 += ncols // 128
                if between is not None:
                    between()
            if ntail:
                tail = []
                for (loader, ncols) in jobs[len(jobs) - ntail:]:
                    tail.append((loader(), ncols, oc0))
                    oc0 += ncols // 128
                    if between is not None:
                        between()
                for tt in range(3):
                    for (s, ncols, ocb) in tail:
                        for ocl in range(ncols // 128):
                            grp(s, ncols, ocb, ocl, tt)
                    on_tile_done(tt)

        def ffn(i, between=None, on_tile_done=None, early_hook=None):
            for half in range(2):
                arena_reset()
                hid = aalloc(BF16, [16, T])
                HID = [[new_res() for _ in range(3)] for _ in range(16)]
                rt = [aalloc(BF16, [512]) for _ in range(3)]
                RT = [new_res() for _ in range(3)]
                rk = [0]
                def grp1(s, j, ocl, tt):
                    hc = j * 4 + ocl
                    b = next_bank()
                    for kc in range(8):
                        P.add("pe", lambda e, kc=kc, b=b: e.matmul(
                            PS[b][:], lhsT=ring[s][:, kc * 512 + ocl * 128:kc * 512 + (ocl + 1) * 128],
                            rhs=hT[:, kc, tile_sl(tt)], start=(kc == 0), stop=(kc == 7)),
                            reads=[SLOT[s], HT[kc][tt]], writes=[PSR[b]])
                    m = rk[0] % 3
                    rk[0] += 1
                    P.add("act", lambda e, m=m, b=b: e.activation(out=rt[m], in_=PS[b][:], func=AF.Relu),
                          reads=[PSR[b]], writes=[RT[m]])
                    P.add("dve", lambda e, m=m: e.tensor_tensor(
                        out=hid[:, hc, tile_sl(tt)], in0=rt[m], in1=rt[m], op=ALU.mult),
                        reads=[RT[m]], writes=[HID[hc][tt]])

                def w1job(j):
                    return wload(wsrc(ff_w1[i], 0, 8, half * 2048 + j * 512, 512), wview(8, 512))
                j = 0
                if half == 0:
                    sa, sb = w1job(0), w1job(1)
                    for tt in range(3):
                        for (s_, j_) in ((sa, 0), (sb, 1)):
                            for ocl in range(4):
                                grp1(s_, j_, ocl, tt)
                        if tt == 0 and early_hook is not None:
                            early_hook()
                    j = 2
                while j < 4:
                    s = w1job(j)
                    for ocl in range(4):
                        for tt in range(3):
                            grp1(s, j, ocl, tt)
                    if between is not None:
                        between()
                    j += 1
                jobs = [((lambda j=j: wload(wsrc(ff_w2[i], half * 2048, 16, j * 256, 256), wview(16, 256))), 256) for j in range(4)]
                out_proj(i, 1, jobs, lambda kc, tt: hid[:, kc, tile_sl(tt)], lambda kc, tt: HID[kc][tt], 16,
                         on_tile_done=(on_tile_done if half == 1 else None), between=between)

        def gmlp(i, jl, on_tile_done=None, early_hook=None, between=None):
            arena_reset()
            wsn = aalloc(F32, [8, 128])
            wsT = aalloc(BF16, [8, 128])
            bsb = aalloc(F32, [8, 128])
            R_wsn, R_wsT, R_bsb = new_res(), new_res(), new_res()
            P.add("sp", lambda e: e.dma_start(out=wsn, in_=aws_d[jl].rearrange("g p q -> p g q")), writes=[R_wsn], dma_key="wsn")
            P.add("sp", lambda e: e.dma_start(out=bsb, in_=bass.AP(abs_d, jl * 1024, [[0, 128], [128, 8], [1, 128]])),
                  writes=[R_bsb], dma_key="bsb")
            for g2 in range(2):
                b = next_bank()
                for g in range(4):
                    P.add("pe", lambda e, g=g, g2=g2, b=b: e.transpose(out=PS[b][:, g * 128:(g + 1) * 128], in_=wsn[:, g2 * 4 + g, :], identity=ident_f[:]),
                          reads=[R_wsn, R_const], writes=[PSR[b]])
                P.add("dve", lambda e, g2=g2, b=b: e.tensor_copy(out=wsT[:, g2 * 4:(g2 + 1) * 4, :], in_=PS[b][:].rearrange("p (g q) -> p g q", g=4)),
                      reads=[PSR[b]], writes=[R_wsT])
            vtok = aalloc(BF16, [12, 2048])
            VT = [[new_res() for _ in range(4)] for _ in range(12)]
            ush = aalloc(BF16, [8, T])
            usb = [wsn[:, 4 * k_:4 * k_ + 4, :].rearrange("p a q -> p (a q)") for k_ in range(2)]
            ssb = [aalloc(F32, [4, 128]) for _ in range(2)]
            USB = [new_res() for _ in range(2)]
            SSB = [new_res() for _ in range(2)]
            ss = aalloc(F32, [12, 4])
            sst = aalloc(F32, [12])
            junk = ssb[0].rearrange("p a q -> p (a q)").bitcast(BF16)[:, 0:512]
            R_ss, R_sst, R_junk = new_res(), new_res(), new_res()
            def v_grp(s, jv, ch):
                b = next_bank()
                tok = slice(ch * 128, (ch + 1) * 128)
                for kc in range(8):
                    P.add("pe", lambda e, kc=kc, b=b: e.matmul(
                        PS[b][:], lhsT=hT[:, kc, tok], rhs=ring[s][:, kc * 512:(kc + 1) * 512],
                        start=(kc == 0), stop=(kc == 7)), reads=[SLOT[s], HT[kc][ch // 4]], writes=[PSR[b]])
                P.add("act", lambda e, b=b: e.activation(
                    out=vtok[:, ch, jv * 512:(jv + 1) * 512], in_=PS[b][:], func=AF.Gelu_apprx_tanh),
                    reads=[PSR[b]], writes=[VT[ch][jv]])
                P.add("dve", lambda e: e.scalar_tensor_tensor(
                    out=junk, in0=vtok[:, ch, jv * 512:(jv + 1) * 512], scalar=1.0, in1=vtok[:, ch, jv * 512:(jv + 1) * 512],
                    op0=ALU.mult, op1=ALU.mult, accum_out=ss[:, ch, jv:jv + 1]),
                    reads=[VT[ch][jv]], writes=[R_junk, R_ss])

            def v_job(jv):
                return wload(wsrc(a_w_in[jl], 0, 8, 2048 + jv * 512, 512), wview(8, 512))
            jv0 = 0
            if between is None:
                sa, sb = v_job(0), v_job(1)
                for ch in range(12):
                    v_grp(sa, 0, ch)
                    v_grp(sb, 1, ch)
                    if ch == 3 and early_hook is not None:
                        early_hook()
                jv0 = 2
            for jv in range(jv0, 4):
                s = v_job(jv)
                for ch in range(12):
                    v_grp(s, jv, ch)
                    if jv == 0 and ch == 3 and early_hook is not None:
                        early_hook()
                if between is not None:
                    between()
            P.add("dve", lambda e: e.reduce_sum(out=sst, in_=ss, axis=mybir.AxisListType.X), reads=[R_ss], writes=[R_sst])
            P.add("act", lambda e: e.activation(out=sst, in_=sst, func=AF.Ln, bias=epsc[:, 0:1], scale=1.0 / 2048.0),
                  reads=[R_sst, R_const], writes=[R_sst])
            P.add("act", lambda e: e.activation(out=sst, in_=sst, func=AF.Exp, scale=-0.5), reads=[R_sst], writes=[R_sst])
            for ch in range(12):
                P.add("dve", lambda e, ch=ch: e.tensor_scalar(
                    out=vtok[:, ch, :], in0=vtok[:, ch, :], scalar1=sst[:, ch:ch + 1], scalar2=None, op0=ALU.mult),
                    reads=[R_sst] + VT[ch], writes=VT[ch])
            for hf in range(2):
                US = [[new_res() for _ in range(3)] for _ in range(8)]
                for ju in range(2 * hf, 2 * hf + 2):
                    if between is not None:
                        between()
                    s = wload(wsrc(a_w_in[jl], 0, 8, ju * 512, 512), wview(8, 512))
                    for cl in range(4):
                        c = ju * 4 + cl
                        g = c // 2
                        vg = vecT[:, R_VGAIN + jl * 16 + c:R_VGAIN + jl * 16 + c + 1]
                        for tt in range(3):
                            bs_ = next_bank()
                            for c4 in range(4):
                                ch = tt * 4 + c4
                                P.add("pe", lambda e, c=c, c4=c4, ch=ch, g=g, bs_=bs_: e.matmul(
                                    PS[bs_][:, c4 * 128:(c4 + 1) * 128], lhsT=vtok[:, ch, c * 128:(c + 1) * 128],
                                    rhs=wsT[:, g, :], start=True, stop=True),
                                    reads=[VT[ch][c // 4], R_wsT], writes=[PSR[bs_]])
                            bu = next_bank()
                            for kc in range(8):
                                P.add("pe", lambda e, s=s, kc=kc, cl=cl, bu=bu, tt=tt: e.matmul(
                                    PS[bu][:], lhsT=ring[s][:, kc * 512 + cl * 128:kc * 512 + (cl + 1) * 128],
                                    rhs=hT[:, kc, tile_sl(tt)], start=(kc == 0), stop=(kc == 7)),
                                    reads=[SLOT[s], HT[kc][tt]], writes=[PSR[bu]])
                            m = state["ev"] % 2
                            state["ev"] += 1
                            P.add("dve", lambda e, m=m, bs_=bs_, g=g, vg=vg: e.scalar_tensor_tensor(
                                out=ssb[m], in0=PS[bs_][:].rearrange("p (a q) -> p a q", a=4), scalar=vg,
                                in1=bc_mid(bsb[:, g, :], 4), op0=ALU.mult, op1=ALU.add),
                                reads=[PSR[bs_], R_vecT, R_bsb], writes=[SSB[m]])
                            P.add("act", lambda e, m=m, bu=bu: e.activation(out=usb[m], in_=PS[bu][:], func=AF.Gelu_apprx_tanh),
                                  reads=[PSR[bu]], writes=[USB[m]])
                            P.add("dve", lambda e, m=m, c=c, tt=tt, hf=hf: e.tensor_tensor(
                                out=ush[:, c - 8 * hf, tile_sl(tt)], in0=usb[m], in1=ssb[m].rearrange("p a q -> p (a q)"), op=ALU.mult),
                                reads=[USB[m], SSB[m]], writes=[US[c - 8 * hf][tt]])
                jobs = [((lambda jo=jo, hf=hf: wload(wsrc(a_w_out[jl], hf * 1024, 8, jo * 512, 512), wview(8, 512))), 512) for jo in range(2)]
                out_proj(i, 0, jobs, lambda kc, tt: ush[:, kc, tile_sl(tt)], lambda kc, tt, US=US: US[kc][tt], 8,
                         on_tile_done=(on_tile_done if hf == 1 else None))

        def conv(i, jl, on_tile_done=None, early_hook=None):
            if early_hook is not None:
                early_hook()
            arena_reset()
            gT = aalloc(BF16, [8, T])
            GT = [[new_res() for _ in range(3)] for _ in range(8)]
            xtb = aalloc(F32, [T])
            zb = aalloc(F32, [T])
            accb = aalloc(F32, [T])
            bgb = aalloc(F32, [T])
            R_bg = [new_res() for _ in range(3)]
            R_xt = [new_res() for _ in range(3)]
            R_z = [new_res() for _ in range(3)]
            R_acc = [new_res() for _ in range(3)]
            w3 = c_w_in.rearrange("(k p) (b n) -> p k b n", p=128, b=3)
            seqs = [(0, 1024), (1024, 256), (1280, 256)]

            def tiles_of(s0, ln):
                return sorted(set([s0 // 512, (s0 + ln - 1) // 512]))

            for fc in range(8):
                s = wload([w3[:, :, b_, fc * 128:(fc + 1) * 128] for b_ in range(3)],
                          lambda t: [t[:, 0:8 * 384].rearrange("p (k b n) -> p k b n", k=8, b=3)[:, :, b_, :] for b_ in range(3)])

                def proj(bsel, tt, s=s):
                    b = next_bank()
                    for kc in range(8):
                        P.add("pe", lambda e, kc=kc, b=b, bsel=bsel, tt=tt: e.matmul(
                            PS[b][:], lhsT=ring[s][:, kc * 384 + bsel * 128:kc * 384 + (bsel + 1) * 128],
                            rhs=hT[:, kc, tile_sl(tt)], start=(kc == 0), stop=(kc == 7)),
                            reads=[SLOT[s], HT[kc][tt]], writes=[PSR[b]])
                    return b
                for tt in range(3):
                    b = proj(2, tt)
                    P.add("act", lambda e, b=b, tt=tt: e.activation(out=xtb[:, tile_sl(tt)], in_=PS[b][:], func=AF.Copy),
                          reads=[PSR[b]], writes=[R_xt[tt]])
                for tt in range(3):
                    b = proj(1, tt)
                    P.add("dve", lambda e, b=b, tt=tt: e.tensor_tensor(out=zb[:, tile_sl(tt)], in0=PS[b][:], in1=xtb[:, tile_sl(tt)], op=ALU.mult),
                          reads=[PSR[b], R_xt[tt]], writes=[R_z[tt]])
                w0 = vecT[:, R_CONVW + 0 * 8 + fc:R_CONVW + 0 * 8 + fc + 1]
                w1 = vecT[:, R_CONVW + 1 * 8 + fc:R_CONVW + 1 * 8 + fc + 1]
                w2 = vecT[:, R_CONVW + 2 * 8 + fc:R_CONVW + 2 * 8 + fc + 1]
                cb = vecT[:, R_CONVB + fc:R_CONVB + fc + 1]
                for tt in range(3):
                    P.add("act", lambda e, tt=tt, w1=w1, cb=cb: e.activation(out=accb[:, tile_sl(tt)], in_=zb[:, tile_sl(tt)], func=AF.Identity, bias=cb, scale=w1),
                          reads=[R_z[tt], R_vecT], writes=[R_acc[tt]])
                for (s0, ln) in seqs:
                    tl = tiles_of(s0, ln)
                    P.add("dve", lambda e, s0=s0, ln=ln, w0=w0: e.scalar_tensor_tensor(
                        out=accb[:, s0 + 1:s0 + ln], in0=zb[:, s0:s0 + ln - 1], scalar=w0, in1=accb[:, s0 + 1:s0 + ln],
                        op0=ALU.mult, op1=ALU.add), reads=[R_z[t_] for t_ in tl] + [R_vecT], writes=[R_acc[t_] for t_ in tl])
                    P.add("dve", lambda e, s0=s0, ln=ln, w2=w2: e.scalar_tensor_tensor(
                        out=accb[:, s0:s0 + ln - 1], in0=zb[:, s0 + 1:s0 + ln], scalar=w2, in1=accb[:, s0:s0 + ln - 1],
                        op0=ALU.mult, op1=ALU.add), reads=[R_z[t_] for t_ in tl] + [R_vecT], writes=[R_acc[t_] for t_ in tl])
                for tt in range(3):
                    b = proj(0, tt)
                    P.add("dve", lambda e, b=b, tt=tt, fc=fc: e.tensor_tensor(out=gT[:, fc, tile_sl(tt)], in0=PS[b][:], in1=accb[:, tile_sl(tt)], op=ALU.mult),
                          reads=[PSR[b], R_acc[tt]], writes=[GT[fc][tt]])
            jobs = [((lambda j=j: wload(wsrc(c_w_out, 0, 8, j * 512, 512), wview(8, 512))), 512) for j in range(2)]
            out_proj(i, 0, jobs, lambda kc, tt: gT[:, kc, tile_sl(tt)], lambda kc, tt: GT[kc][tt], 8, on_tile_done=on_tile_done)

        def attention(i, jl, on_tile_done=None, early_hook=None):
            if early_hook is not None:
                early_hook()
            arena_reset()
            attnT = aalloc(BF16, [8, T])
            AT = [[new_res() for _ in range(3)] for _ in range(8)]
            qT = [aalloc(BF16, [T]) for _ in range(2)]
            kT = [aalloc(BF16, [T]) for _ in range(2)]
            vtk = [aalloc(BF16, [12, 128]) for _ in range(2)]
            kcT = [aalloc(BF16, [512]) for _ in range(2)]
            vc = [aalloc(BF16, [4, 128]) for _ in range(2)]
            ckst = [aalloc(F32, [4, 128]) for _ in range(2)]
            cvst = [aalloc(F32, [4, 128]) for _ in range(2)]
            eraw = [aalloc(F32, [16, 64]) for _ in range(2)]
            ET = [[aalloc(BF16, [16, 64]) for _ in range(2)] for _ in range(2)]
            cmk = aalloc(F32, [64])
            Pb = [aalloc(BF16, [512]) for _ in range(6)]
            sqn = [aalloc(BF16, [512]) for _ in range(2)]
            rsn1 = aalloc(F32, [512])
            rsn = [rsn1, rsn1]
            kn32 = aalloc(F32, [512])
            nkst = aalloc(F32, [4, 128])
            nvst = aalloc(F32, [4, 128])
            rden = [aalloc(F32, [512]) for _ in range(1)]
            gqs = aalloc(F32, [1])
            R_cmk, R_gqs = new_res(), new_res()
            R_QT = [[new_res() for _ in range(3)] for _ in range(2)]
            R_KT = [[new_res() for _ in range(3)] for _ in range(2)]
            R_VTK = [[new_res() for _ in range(3)] for _ in range(2)]
            R_kcT = [new_res() for _ in range(2)]
            R_vc = [new_res() for _ in range(2)]
            R_ckst = [new_res() for _ in range(2)]
            R_cvst = [new_res() for _ in range(2)]
            R_eraw = [new_res() for _ in range(2)]
            R_ET = [[new_res() for _ in range(2)] for _ in range(2)]
            R_Pb = [new_res() for _ in range(6)]
            R_sqn = [new_res() for _ in range(2)]
            R_rsn1 = new_res()
            R_rsn = [R_rsn1, R_rsn1]
            R_kn32, R_nkst, R_nvst = new_res(), new_res(), new_res()
            R_rden = [new_res() for _ in range(1)]
            cnt = {"pb": 0, "nrm": 0, "et": 0, "rd": 0}
            P.add("sp", lambda e: e.dma_start(out=cmk, in_=cmask_d), writes=[R_cmk], dma_key="cmk")
            P.add("dve", lambda e: e.tensor_scalar(out=gqs, in0=vecT[:, R_QG:R_QG + 1], scalar1=0.125, scalar2=None, op0=ALU.mult),
                  reads=[R_vecT], writes=[R_gqs])
            gk = vecT[:, R_KG:R_KG + 1]
            w3 = b_w_qkv.rearrange("(k p) (b n) -> p k b n", p=128, b=3)
            nk_v = nk_d.rearrange("(c p) f -> p c f", p=128)
            nv_v = nv_d.rearrange("(c p) f -> p c f", p=128)
            out_dmas = []

            def qrange(kt):
                if kt <= 3:
                    return 0, 2 * kt + 5, (0, 2 * kt + 5)
                return 2 * kt - 3, 15, (1, 2 * kt - 3)

            wslot = {}

            def wreq(hp):
                if hp < 8 and hp not in wslot:
                    wslot[hp] = wload([w3[:, :, b_, hp * 128:(hp + 1) * 128] for b_ in range(3)],
                                      lambda t: [t[:, 0:8 * 384].rearrange("p (k b n) -> p k b n", k=8, b=3)[:, :, b_, :] for b_ in range(3)])

            def setup(hp):
                pp = hp % 2
                wreq(hp)
                wreq(hp + 1)
                s = wslot[hp]
                for a_h in range(2):
                    h = hp * 2 + a_h
                    for aa in range(2):
                        t0 = 1 - aa
                        src = bass.AP(rp_d, h * 17 * 127 + t0 * 127, [[1, 64], [127, 16], [1, 64]])
                        P.add("sp", lambda e, a_h=a_h, aa=aa, src=src: e.dma_start(out=eraw[a_h][aa * 64:(aa + 1) * 64, :, :], in_=src),
                              writes=[R_eraw[a_h]], dma_key="eraw%d" % a_h)
                P.add("sp", lambda e, pp=pp, hp=hp: e.dma_start(out=ckst[pp], in_=ck_d.rearrange("(t p) f -> p t f", p=128)[:, :, hp * 128:(hp + 1) * 128]),
                      writes=[R_ckst[pp]], dma_key="ckst%d" % pp)
                P.add("sp", lambda e, pp=pp, hp=hp: e.dma_start(out=cvst[pp], in_=cv_d.rearrange("(t p) f -> p t f", p=128)[:, :, hp * 128:(hp + 1) * 128]),
                      writes=[R_cvst[pp]], dma_key="cvst%d" % pp)
                yield
                for bsel in range(2):
                    for tt in range(3):
                        b = next_bank()
                        for kc in range(8):
                            P.add("pe", lambda e, kc=kc, b=b, bsel=bsel, tt=tt, s=s: e.matmul(
                                PS[b][:], lhsT=ring[s][:, kc * 384 + bsel * 128:kc * 384 + (bsel + 1) * 128],
                                rhs=hT[:, kc, tile_sl(tt)], start=(kc == 0), stop=(kc == 7)),
                                reads=[SLOT[s], HT[kc][tt]], writes=[PSR[b]])
                        m = cnt["nrm"] % 2
                        cnt["nrm"] += 1
                        P.add("act", lambda e, m=m, b=b: e.activation(out=sqn[m], in_=PS[b][:], func=AF.Square), reads=[PSR[b]], writes=[R_sqn[m]])
                        b2 = next_bank()
                        P.add("pe", lambda e, m=m, b2=b2: e.matmul(PS[b2][:], lhsT=blk64[:], rhs=sqn[m], start=True, stop=True),
                              reads=[R_sqn[m], R_const], writes=[PSR[b2]])
                        P.add("act", lambda e, m=m, b2=b2: e.activation(out=rsn[m], in_=PS[b2][:], func=AF.Ln, bias=epsc[:, 0:1], scale=1.0),
                              reads=[PSR[b2], R_const], writes=[R_rsn[m]])
                        P.add("act", lambda e, m=m: e.activation(out=rsn[m], in_=rsn[m], func=AF.Exp, scale=-0.5), reads=[R_rsn[m]], writes=[R_rsn[m]])
                        tsl = tile_sl(tt)
                        if bsel == 0:
                            P.add("dve", lambda e, m=m, b=b, pp=pp, tsl=tsl: e.scalar_tensor_tensor(
                                out=qT[pp][:, tsl], in0=PS[b][:], scalar=gqs[:, 0:1], in1=rsn[m], op0=ALU.mult, op1=ALU.mult),
                                reads=[PSR[b], R_rsn[m], R_gqs], writes=[R_QT[pp][tt]])
                        elif tt < 2:
                            P.add("dve", lambda e, m=m, b=b, pp=pp, tsl=tsl: e.scalar_tensor_tensor(
                                out=kT[pp][:, tsl], in0=PS[b][:], scalar=gk, in1=rsn[m], op0=ALU.mult, op1=ALU.mult),
                                reads=[PSR[b], R_rsn[m], R_vecT], writes=[R_KT[pp][tt]])
                        else:
                            P.add("dve", lambda e, m=m, b=b: e.scalar_tensor_tensor(
                                out=kn32, in0=PS[b][:], scalar=gk, in1=rsn[m], op0=ALU.mult, op1=ALU.mult),
                                reads=[PSR[b], R_rsn[m], R_vecT], writes=[R_kn32])
                            P.add("act", lambda e, pp=pp, tsl=tsl: e.activation(out=kT[pp][:, tsl], in_=kn32, func=AF.Copy),
                                  reads=[R_kn32], writes=[R_KT[pp][tt]])
                            b3 = next_bank()
                            for c4 in range(4):
                                P.add("pe", lambda e, c4=c4, b3=b3: e.transpose(out=PS[b3][:, c4 * 128:(c4 + 1) * 128], in_=kn32[:, c4 * 128:(c4 + 1) * 128], identity=ident_f[:]),
                                      reads=[R_kn32, R_const], writes=[PSR[b3]])
                            P.add("dve", lambda e, b3=b3: e.tensor_copy(out=nkst, in_=PS[b3][:].rearrange("p (c f) -> p c f", c=4)),
                                  reads=[PSR[b3]], writes=[R_nkst])
                            out_dmas.append(P.add("sp", lambda e, hp=hp: e.dma_start(out=nk_v[:, :, hp * 128:(hp + 1) * 128], in_=nkst),
                                                  reads=[R_nkst], dma_key="nkst"))
                        yield
                b = next_bank()
                for t4 in range(4):
                    P.add("pe", lambda e, pp=pp, t4=t4, b=b: e.transpose(out=PS[b][:, t4 * 128:(t4 + 1) * 128], in_=ckst[pp][:, t4, :], identity=ident_f[:]),
                          reads=[R_ckst[pp], R_const], writes=[PSR[b]])
                P.add("dve", lambda e, pp=pp, b=b: e.tensor_copy(out=kcT[pp], in_=PS[b][:]), reads=[PSR[b]], writes=[R_kcT[pp]])
                P.add("pool", lambda e, pp=pp: e.tensor_copy(out=vc[pp], in_=cvst[pp]), reads=[R_cvst[pp]], writes=[R_vc[pp]])
                yield
                for c3 in range(3):
                    b = next_bank()
                    for c4 in range(4):
                        c = c3 * 4 + c4
                        for kc in range(8):
                            P.add("pe", lambda e, kc=kc, b=b, c=c, c4=c4, s=s: e.matmul(
                                PS[b][:, c4 * 128:(c4 + 1) * 128], lhsT=hT[:, kc, c * 128:(c + 1) * 128],
                                rhs=ring[s][:, kc * 384 + 256:kc * 384 + 384], start=(kc == 0), stop=(kc == 7)),
                                reads=[SLOT[s], HT[kc][c3]], writes=[PSR[b]])
                    P.add("dve", lambda e, pp=pp, c3=c3, b=b: e.tensor_copy(out=vtk[pp][:, c3 * 4:(c3 + 1) * 4, :], in_=PS[b][:].rearrange("p (c f) -> p c f", c=4)),
                          reads=[PSR[b]], writes=[R_VTK[pp][c3]])
                    if c3 == 2:
                        P.add("dve", lambda e, b=b: e.tensor_copy(out=nvst, in_=PS[b][:].rearrange("p (c f) -> p c f", c=4)),
                              reads=[PSR[b]], writes=[R_nvst])
                        out_dmas.append(P.add("sp", lambda e, hp=hp: e.dma_start(out=nv_v[:, :, hp * 128:(hp + 1) * 128], in_=nvst),
                                              reads=[R_nvst], dma_key="nvst"))
                    yield
                for a_h in range(2):
                    P.add("act", lambda e, a_h=a_h: e.activation(out=eraw[a_h], in_=eraw[a_h], func=AF.Exp), reads=[R_eraw[a_h]], writes=[R_eraw[a_h]])
                    P.add("dve", lambda e, a_h=a_h, pp=pp: e.tensor_tensor(out=ET[pp][a_h], in0=eraw[a_h], in1=bc_mid(cmk, 16), op=ALU.mult),
                          reads=[R_eraw[a_h], R_cmk], writes=[R_ET[pp][a_h]])
                    yield

            def core(hp, gen):
                pp = hp % 2
                tiles = []
                groups = []

                def add_group(cols, tt_list):
                    g_ = {"bn": None, "bd": None, "cols": cols, "tt": tt_list}
                    groups.append(g_)
                    return g_

                for qt in range(2):
                    g_ = add_group(slice(qt * 512, (qt + 1) * 512), [qt])
                    for kt in range(4):
                        for a_h in range(2):
                            hb = a_h * 64
                            tiles.append(dict(g=g_, lhsT=kcT[pp][hb:hb + 64, kt * 128:(kt + 1) * 128], rhs=qT[pp][hb:hb + 64, qt * 512:(qt + 1) * 512],
                                              n=512, reads=[R_kcT[pp], R_QT[pp][qt]], et=None, zero=None, hb=hb, c0=0,
                                              vlhs=vc[pp][:, kt, hb:hb + 64], vreads=[R_vc[pp]], first=(kt == 0)))
                    for kt in range(8):
                        qlo, qhi, (ih, irow) = qrange(kt)
                        r0 = max(qlo, 8 * qt)
                        r1 = min(qhi, 8 * qt + 7)
                        if r1 < r0:
                            continue
                        nr = r1 - r0 + 1
                        j0 = r0 - 2 * kt + 7
                        for a_h in range(2):
                            hb = a_h * 64
                            v_ = pp
                            ea = ET[pp][a_h]
                            e_in = bass.AP(ea.tensor, ea.offset + j0 * 64 + 63, [list(ea.ap[0]), [64, nr], [-1, 64]])
                            zero = (ih, (irow - r0) * 64) if r0 <= irow <= r1 else None
                            tiles.append(dict(g=g_, lhsT=kT[pp][hb:hb + 64, kt * 128:(kt + 1) * 128], rhs=qT[pp][hb:hb + 64, r0 * 64:(r1 + 1) * 64],
                                              n=nr * 64, reads=[R_KT[pp][kt // 4], R_QT[pp][qt]], et=(e_in, nr, a_h, v_), zero=zero, hb=hb,
                                              c0=(r0 - 8 * qt) * 64, vlhs=vtk[pp][:, kt, hb:hb + 64], vreads=[R_VTK[pp][kt // 4]], first=False))
                g_ = add_group(slice(1024, 1536), [2])
                for sq_ in range(2):
                    t0 = 1024 + 256 * sq_
                    for kt in range(2):
                        for a_h in range(2):
                            hb = a_h * 64
                            tiles.append(dict(g=g_, lhsT=kT[pp][hb:hb + 64, t0 + kt * 128:t0 + (kt + 1) * 128], rhs=qT[pp][hb:hb + 64, t0:t0 + 256],
                                              n=256, reads=[R_KT[pp][2], R_QT[pp][2]], et=None, zero=None, hb=hb, c0=sq_ * 256,
                                              vlhs=vtk[pp][:, 8 + 2 * sq_ + kt, hb:hb + 64], vreads=[R_VTK[pp][2]], first=(sq_ == 0 and kt == 0)))
                for idx_, t_ in enumerate(tiles):
                    t_["last"] = (idx_ + 1 == len(tiles)) or (tiles[idx_ + 1]["g"] is not t_["g"])
                LAGP = 2
                npair = len(tiles) // 2
                assert len(tiles) % 2 == 0
                for step in range(npair + LAGP):
                    if step % 2 == 1:
                        next(gen, None)
                    if step < npair:
                        pend = []
                        for t_ in (tiles[2 * step], tiles[2 * step + 1]):
                            g_ = t_["g"]
                            if g_["bn"] is None:
                                g_["bn"] = next_bank()
                                state["reserved"].add(g_["bn"])
                                g_["bd"] = next_bank()
                                state["reserved"].add(g_["bd"])
                            n = t_["n"]
                            bsx = next_bank()
                            P.add("pe", lambda e, bsx=bsx, t_=t_, n=n: e.matmul(PS[bsx][:, 0:n], lhsT=t_["lhsT"], rhs=t_["rhs"], start=True, stop=True),
                                  reads=t_["reads"], writes=[PSR[bsx]])
                            m = cnt["pb"] % 6
                            cnt["pb"] += 1
                            t_["m"] = m
                            pend.append((t_, bsx, m, n))
                        for (t_, bsx, m, n) in pend:
                            P.add("act", lambda e, m=m, bsx=bsx, n=n: e.activation(out=Pb[m][:, 0:n], in_=PS[bsx][:, 0:n], func=AF.Exp),
                                  reads=[PSR[bsx]], writes=[R_Pb[m]])
                            if t_["et"] is not None:
                                e_in, nr, a_h, v_ = t_["et"]
                                P.add("dve", lambda e, m=m, n=n, nr=nr, e_in=e_in: e.tensor_tensor(
                                    out=Pb[m][:, 0:n].rearrange("p (r c) -> p r c", r=nr), in0=Pb[m][:, 0:n].rearrange("p (r c) -> p r c", r=nr),
                                    in1=e_in, op=ALU.mult), reads=[R_Pb[m], R_ET[v_][a_h]], writes=[R_Pb[m]])
                            if t_["zero"] is not None:
                                ih, zc = t_["zero"]
                                P.add("dve", lambda e, m=m, ih=ih, zc=zc: e.memset(Pb[m][ih * 64:(ih + 1) * 64, zc:zc + 64], 0.0), writes=[R_Pb[m]])
                    j_ = step - LAGP
                    if j_ >= 0:
                        pair = (tiles[2 * j_], tiles[2 * j_ + 1])
                        for t_ in pair:
                            g_ = t_["g"]
                            m, n, hb, c0, bn = t_["m"], t_["n"], t_["hb"], t_["c0"], g_["bn"]
                            P.add("pe", lambda e, m=m, n=n, bn=bn, hb=hb, c0=c0, t_=t_: e.matmul(
                                PS[bn][hb:hb + 64, c0:c0 + n], lhsT=t_["vlhs"], rhs=Pb[m][:, 0:n], start=t_["first"], stop=False, skip_group_check=True),
                                reads=[R_Pb[m]] + t_["vreads"], writes=[PSR[bn]])
                        for t_ in pair:
                            g_ = t_["g"]
                            m, n, hb, c0, bd = t_["m"], t_["n"], t_["hb"], t_["c0"], g_["bd"]
                            P.add("pe", lambda e, m=m, n=n, bd=bd, hb=hb, c0=c0, t_=t_: e.matmul(
                                PS[bd][hb:hb + 64, c0:c0 + n], lhsT=ones64[:], rhs=Pb[m][:, 0:n], start=t_["first"], stop=False, skip_group_check=True),
                                reads=[R_Pb[m], R_const], writes=[PSR[bd]])
                        t_ = pair[1]
                        g_ = t_["g"]
                        bn, bd = g_["bn"], g_["bd"]
                        if t_["last"]:
                            cols = g_["cols"]
                            mr = 0
                            cnt["rd"] += 1
                            nn = cols.stop - cols.start
                            P.add("act", lambda e, mr=mr, bd=bd, nn=nn: e.activation(out=rden[mr][:, 0:nn], in_=PS[bd][:, 0:nn], func=AF.Ln), reads=[PSR[bd]], writes=[R_rden[mr]])
                            P.add("act", lambda e, mr=mr, nn=nn: e.activation(out=rden[mr][:, 0:nn], in_=rden[mr][:, 0:nn], func=AF.Exp, scale=-1.0), reads=[R_rden[mr]], writes=[R_rden[mr]])
                            P.add("dve", lambda e, mr=mr, bn=bn, nn=nn, cols=cols, hp=hp: e.tensor_tensor(
                                out=attnT[:, hp, cols], in0=PS[bn][:, 0:nn], in1=rden[mr][:, 0:nn], op=ALU.mult),
                                reads=[PSR[bn], R_rden[mr]], writes=[AT[hp][t2_] for t2_ in g_["tt"]])
                            state["reserved"].discard(bn)
                            state["reserved"].discard(bd)
            for _ in setup(0):
                pass
            for hp in range(8):
                gen = setup(hp + 1) if hp + 1 < 8 else iter(())
                core(hp, gen)
                for _ in gen:
                    pass
            jobs = [((lambda j=j: wload(wsrc(b_w_o, 0, 8, j * 512, 512), wview(8, 512))), 512) for j in range(2)]
            out_proj(i, 0, jobs, lambda kc, tt: attnT[:, kc, tile_sl(tt)], lambda kc, tt: AT[kc][tt], 8, on_tile_done=on_tile_done)
            return out_dmas

        ost = [arena[:, (ARENA_BYTES - 8192 + 4096 * k_) // 2:(ARENA_BYTES - 4096 + 4096 * k_) // 2].bitcast(F32) for k_ in range(2)]
        OST = [Res() for _ in range(2)]
        outs = []

        def out_tile(tt):
            for c in range(tt * 4, tt * 4 + 4):
                k = c % 2
                for g in range(2):
                    b = next_bank()
                    for j in range(4):
                        fc = g * 4 + j
                        P.add("pe", lambda e, c=c, fc=fc, b=b, j=j: e.transpose(
                            out=PS[b][:, j * 128:(j + 1) * 128], in_=xres[:, fc, c * 128:(c + 1) * 128], identity=ident_f[:]),
                            reads=[XR[fc][tt], R_const], writes=[PSR[b]])
                    eng = evac_eng()
                    outap = ost[k][:, g * 512:(g + 1) * 512]
                    if eng == "act":
                        P.add("act", lambda e, outap=outap, b=b: e.activation(out=outap, in_=PS[b][:], func=AF.Copy),
                              reads=[PSR[b]], writes=[OST[k]])
                    else:
                        P.add("dve", lambda e, outap=outap, b=b: e.tensor_copy(out=outap, in_=PS[b][:]),
                              reads=[PSR[b]], writes=[OST[k]])
                dst = ys_d[c * 128:(c + 1) * 128, :] if c < 8 else yp_d[(c - 8) * 128:(c - 7) * 128, :]
                outs.append(P.add("sp", lambda e, k=k, dst=dst: e.dma_start(out=dst, in_=ost[k]), reads=[OST[k]], dma_key="ost%d" % k))

        class OutSched:
            def __init__(self):
                self.prev = None

            def tile_done(self, tt):
                if self.prev is not None:
                    out_tile(self.prev)
                self.prev = tt

            def flush(self):
                if self.prev is not None:
                    out_tile(self.prev)
                    self.prev = None

        extra_outs = []
        first_pending = list(range(12))
        i0 = LAYERS[0][0]
        if LAYERS[0][1] == 0:
            modulation(i0, first_pending[:4])
            first_pending = first_pending[4:]
        else:
            modulation(i0, first_pending)
            first_pending = []

        def first_between():
            if first_pending:
                modulation(i0, [first_pending.pop(0)])
        sched0 = NormSched(i0, 0)
        load_x(sched0.tile_done)
        sched1 = sched0
        for li, (i, kind, jl) in enumerate(LAYERS):
            sched2 = NormSched(i, 1)
            eh = sched1.flush if sched1 is not None else None
            if kind == 0:
                gmlp(i, jl, on_tile_done=sched2.tile_done, early_hook=eh, between=(first_between if li == 0 else None))
                assert li != 0 or not first_pending
            elif kind == 1:
                extra_outs += attention(i, jl, on_tile_done=sched2.tile_done, early_hook=eh)
            else:
                conv(i, jl, on_tile_done=sched2.tile_done, early_hook=eh)
            if li + 1 < len(LAYERS):
                pending = list(range(12))
                nxt = LAYERS[li + 1][0]

                def between(nxt=nxt, pending=pending):
                    if pending:
                        j = pending.pop(0)
                        modulation(nxt, [j])
                sched1 = NormSched(nxt, 0)
                ffn(i, between, on_tile_done=sched1.tile_done, early_hook=sched2.flush)
                assert not pending
            else:
                sched1 = None
                osched = OutSched()
                ffn(i, None, on_tile_done=osched.tile_done, early_hook=sched2.flush)
                osched.flush()

        P.add("sp", None, deps=outs + extra_outs)
        P.emit(nc, st)
    return nc


def make_in_maps(inp):
    f = lambda a: np.ascontiguousarray(np.asarray(a, dtype=np.float32))
    x_prompt, x_sample = f(inp["x_prompt"]), f(inp["x_sample"])
    cache_k, cache_v = f(inp["cache_k"]), f(inp["cache_v"])
    c, c_ctx = f(inp["c"]), f(inp["c_ctx"])
    shared = {
        "a_bs": f(inp["a_bs"]).reshape(2, 1024),
        "a_ws": f(inp["a_ws"]),
        "ident": np.eye(128, dtype=np.float32),
        "ada_w": f(inp["ada_w"]), "a_w_in": f(inp["a_w_in"]), "a_w_out": f(inp["a_w_out"]),
        "b_w_qkv": f(inp["b_w_qkv"])[0], "b_w_o": f(inp["b_w_o"])[0],
        "c_w_in": f(inp["c_w_in"])[0], "c_w_out": f(inp["c_w_out"])[0],
        "ff_w1": f(inp["ff_w1"]), "ff_w2": f(inp["ff_w2"]),
    }
    rpb = f(inp["b_rpb"])[0]
    rp = np.full((16, 17, 127), FILL, np.float32)
    rp[:, 1:16, 48:79] = rpb[:, ::-1, :]
    shared["rp"] = rp
    col = np.arange(64)
    cstart = np.clip(col - 8, 0, 48)
    ok = (col[None, :] >= cstart[:, None]) & (col[None, :] < cstart[:, None] + 16)
    m = ok.T[:, ::-1].astype(np.float32)
    shared["cmask"] = np.ascontiguousarray(np.concatenate([m, m], axis=0))
    vec_common = np.zeros((NVEC, 128), np.float32)
    vec_common[R_ADAB:R_ADAB + 192] = f(inp["ada_b"]).reshape(192, 128)
    vec_common[R_NORMG:R_NORMG + 64] = f(inp["norm_g"]).reshape(64, 128)
    vec_common[R_VGAIN:R_VGAIN + 32] = f(inp["a_v_gain"]).reshape(32, 128)
    vec_common[R_CONVW:R_CONVW + 24] = f(inp["c_conv_w"]).reshape(24, 128)
    vec_common[R_CONVB:R_CONVB + 8] = f(inp["c_conv_b"]).reshape(8, 128)
    vec_common[R_QG] = np.tile(f(inp["b_q_gain"])[0], 2)
    vec_common[R_KG] = np.tile(f(inp["b_k_gain"])[0], 2)
    maps = []
    for i in range(NCORES):
        v = vec_common.copy()
        cond = np.stack([c[i].reshape(8, 128), c_ctx.reshape(8, 128)], axis=1).reshape(16, 128)
        v[R_COND:R_COND + 16] = cond
        m_ = dict(shared)
        m_["xs"] = x_sample[i]
        m_["xp"] = x_prompt[2 * i:2 * i + 2].reshape(512, D)
        m_["ck"] = cache_k[i, 0].reshape(512, D)
        m_["cv"] = cache_v[i, 0].reshape(512, D)
        m_["vecs"] = v
        maps.append(m_)
    return maps


_NC_CACHE = {}


FULL = ((0, 0, 0), (1, 1, 0), (2, 2, 0), (3, 0, 1))


def run(inputs, LAYERS=FULL, trace=False):
    LAYERS = tuple(tuple(x) for x in LAYERS)
    if LAYERS not in _NC_CACHE:
        _NC_CACHE[LAYERS] = build(LAYERS)
    nc = _NC_CACHE[LAYERS]
    maps = make_in_maps(inputs)
    res = run_bass_kernel_spmd(nc, maps, core_ids=list(range(NCORES)), trace=trace)
    r = res.results
    ys = np.stack([r[i]["ys"] for i in range(NCORES)], axis=0)
    yp = np.concatenate([r[i]["yp"].reshape(2, 256, D) for i in range(NCORES)], axis=0)
    nk = np.concatenate([r[i]["nk"].reshape(2, 1, 256, 16, 64) for i in range(NCORES)], axis=0)
    nv = np.concatenate([r[i]["nv"].reshape(2, 1, 256, 16, 64) for i in range(NCORES)], axis=0)
    return (yp, ys, nk, nv), res


def kernel(**inputs):
    out, _ = run(inputs, FULL)
    return out
```
